# Optimizing a Trainium2 kernel written in Bass

```python
import math
import jax, jax.numpy as jnp
from jax import lax
import numpy as np

D_MODEL = 1024
BATCH = 8
SEQ = 2048
DEPTH = 2

N_META = 16
BLOCK = 128
MLA_HEADS = 4
Q_LORA = 256
KV_LORA = 256
QK_NOPE = 128
QK_ROPE = 64
QK_HEAD = QK_NOPE + QK_ROPE
V_HEAD = 128
MLA_WIDTH = MLA_HEADS * V_HEAD
RET_HEADS = 4
RET_HEAD = 128
RET_WIDTH = RET_HEADS * RET_HEAD
MIX_WIDTH = MLA_WIDTH + RET_WIDTH
IN_SIZES = (Q_LORA, KV_LORA, QK_ROPE, RET_WIDTH, RET_WIDTH, RET_WIDTH, RET_WIDTH)
N_IN = sum(IN_SIZES)
D_FF = -(-(8 * D_MODEL) // (3 * 256)) * 256
ROPE_BASE = 10000.0
EPS = 1e-6
NEG_INF = -1e30

kernel_name = "hybrid_mla_retention_swiglu"


def rms_norm(x, g):
    xf = x.astype(jnp.float32)
    y = xf * lax.rsqrt(jnp.mean(xf * xf, axis=-1, keepdims=True) + EPS)
    return (y * g.astype(jnp.float32)).astype(x.dtype)


def rope_tables(n_pos, dim):
    inv = ROPE_BASE ** (-jnp.arange(0, dim, 2, dtype=jnp.float32) / dim)
    ang = jnp.arange(n_pos, dtype=jnp.float32)[:, None] * inv[None, :]
    return jnp.cos(ang), jnp.sin(ang)


def apply_rope(x, cos, sin):
    x1, x2 = jnp.split(x, 2, axis=-1)
    c = cos[:, None, :].astype(x.dtype)
    s = sin[:, None, :].astype(x.dtype)
    return jnp.concatenate([x1 * c - x2 * s, x2 * c + x1 * s], axis=-1)


def mla_mixer(c_q, c_kv, k_pe, q_a_g, w_q_b, kv_a_g, w_kv_b, q_g, k_g, out_g, cos, sin):
    B, L, _ = c_q.shape
    q = (rms_norm(c_q, q_a_g) @ w_q_b).reshape(B, L, MLA_HEADS, QK_HEAD)
    kv = (rms_norm(c_kv, kv_a_g) @ w_kv_b).reshape(B, L, MLA_HEADS, QK_NOPE + V_HEAD)
    k_nope, v = kv[..., :QK_NOPE], kv[..., QK_NOPE:]
    k_pe_h = jnp.broadcast_to(k_pe[:, :, None, :], (B, L, MLA_HEADS, QK_ROPE))
    k = jnp.concatenate([k_nope, k_pe_h], axis=-1)
    q = rms_norm(q, q_g)
    k = rms_norm(k, k_g)
    q = jnp.concatenate([q[..., :QK_NOPE], apply_rope(q[..., QK_NOPE:], cos, sin)], axis=-1)
    k = jnp.concatenate([k[..., :QK_NOPE], apply_rope(k[..., QK_NOPE:], cos, sin)], axis=-1)
    scale = QK_HEAD ** -0.5
    bounds = [0] + [N_META + BLOCK * j for j in range((L - N_META) // BLOCK + 1)]
    outs = []
    for start, end in zip(bounds[:-1], bounds[1:]):
        qb, kb, vb = q[:, start:end], k[:, :end], v[:, :end]
        s = jnp.einsum('bqhd,bkhd->bhqk', qb, kb).astype(jnp.float32) * scale
        causal = jnp.arange(start, end)[:, None] >= jnp.arange(end)[None, :]
        s = jnp.where(causal[None, None], s, NEG_INF)
        p = jax.nn.softmax(s, axis=-1).astype(vb.dtype)
        outs.append(jnp.einsum('bhqk,bkhd->bqhd', p, vb))
    o = jnp.concatenate(outs, axis=1)
    o = rms_norm(o, out_g.reshape(MLA_HEADS, V_HEAD))
    return o.reshape(B, L, MLA_WIDTH)


def retention_mixer(rq, rk, rv, rg, norm_g, norm_b, cos, sin):
    B, L, _ = rq.shape
    dt = rq.dtype
    q = apply_rope(rq.reshape(B, L, RET_HEADS, RET_HEAD), cos, sin).astype(jnp.float32)
    k = (apply_rope(rk.reshape(B, L, RET_HEADS, RET_HEAD), cos, sin) * RET_HEAD ** -0.5).astype(jnp.float32)
    v = rv.reshape(B, L, RET_HEADS, RET_HEAD).astype(jnp.float32)
    pad = BLOCK - N_META
    padw = ((0, 0), (pad, 0), (0, 0), (0, 0))
    q, k, v = (jnp.pad(t, padw) for t in (q, k, v))
    Lp = L + pad
    n_chunks = Lp // BLOCK

    def to_chunks(t):
        return t.reshape(B, n_chunks, BLOCK, RET_HEADS, RET_HEAD).transpose(1, 0, 3, 2, 4)

    qc, kc, vc = to_chunks(q), to_chunks(k), to_chunks(v)
    gamma = 1.0 - 2.0 ** (-5.0 - jnp.arange(RET_HEADS, dtype=jnp.float32))
    log_g = jnp.log(gamma)
    idx = jnp.arange(BLOCK, dtype=jnp.float32)
    diff = idx[:, None] - idx[None, :]
    decay = jnp.where(diff >= 0, jnp.exp(jnp.maximum(diff, 0.0)[None] * log_g[:, None, None]), 0.0)
    xi = jnp.exp((idx + 1.0)[None, :] * log_g[:, None])
    zeta = jnp.exp((BLOCK - 1.0 - idx)[None, :] * log_g[:, None])
    chunk_decay = jnp.exp(BLOCK * log_g)

    def step(state, inp):
        qb, kb, vb = inp
        s = jnp.einsum('bhnd,bhmd->bhnm', qb, kb) * decay[None]
        inner = jnp.einsum('bhnm,bhmd->bhnd', s, vb)
        cross = jnp.einsum('bhnd,bhde->bhne', qb, state) * xi[None, :, :, None]
        new_state = state * chunk_decay[None, :, None, None] + jnp.einsum(
            'bhmd,bhme->bhde', kb * zeta[None, :, :, None], vb)
        return new_state, inner + cross

    state0 = jnp.zeros((B, RET_HEADS, RET_HEAD, RET_HEAD), jnp.float32)
    _, o = lax.scan(step, state0, (qc, kc, vc))
    o = o.transpose(1, 0, 3, 2, 4).reshape(B, Lp, RET_HEADS, RET_HEAD)[:, pad:]
    mu = jnp.mean(o, axis=-1, keepdims=True)
    var = jnp.mean(jnp.square(o - mu), axis=-1, keepdims=True)
    o = ((o - mu) * lax.rsqrt(var + EPS)).reshape(B, L, RET_WIDTH)
    o = o * norm_g.astype(jnp.float32) + norm_b.astype(jnp.float32)
    return (jax.nn.silu(rg.astype(jnp.float32)) * o).astype(dt)


def setup_inputs(seed: int = 0) -> dict:
    key = jax.random.key(seed)
    ks = jax.random.split(key, 20)

    def w(k, shape, fan_in):
        return jax.random.normal(k, shape, jnp.float32) * fan_in ** -0.5

    def gain(k, shape):
        return 1.0 + 0.02 * jax.random.normal(k, shape, jnp.float32)

    return {
        "x": jax.random.normal(ks[0], (BATCH, SEQ, D_MODEL), jnp.float32),
        "meta_tokens": jax.random.normal(ks[1], (N_META, D_MODEL), jnp.float32),
        "attn_norm_g": gain(ks[2], (DEPTH, D_MODEL)),
        "w_in": w(ks[3], (DEPTH, D_MODEL, N_IN), D_MODEL),
        "q_a_norm_g": gain(ks[4], (DEPTH, Q_LORA)),
        "w_q_b": w(ks[5], (DEPTH, Q_LORA, MLA_HEADS * QK_HEAD), Q_LORA),
        "kv_a_norm_g": gain(ks[6], (DEPTH, KV_LORA)),
        "w_kv_b": w(ks[7], (DEPTH, KV_LORA, MLA_HEADS * (QK_NOPE + V_HEAD)), KV_LORA),
        "q_norm_g": gain(ks[8], (DEPTH, QK_HEAD)),
        "k_norm_g": gain(ks[9], (DEPTH, QK_HEAD)),
        "mla_out_norm_g": gain(ks[10], (DEPTH, MLA_WIDTH)),
        "ret_norm_g": gain(ks[11], (DEPTH, RET_WIDTH)),
        "ret_norm_b": 0.02 * jax.random.normal(ks[12], (DEPTH, RET_WIDTH), jnp.float32),
        "w_out": w(ks[13], (DEPTH, MIX_WIDTH, D_MODEL), MIX_WIDTH),
        "ffn_norm_g": gain(ks[14], (DEPTH, D_MODEL)),
        "w_gate_up": w(ks[15], (DEPTH, D_MODEL, 2 * D_FF), D_MODEL),
        "w_down": w(ks[16], (DEPTH, D_FF, D_MODEL), D_FF),
    }


def reference(x, meta_tokens, attn_norm_g, w_in, q_a_norm_g, w_q_b, kv_a_norm_g, w_kv_b,
              q_norm_g, k_norm_g, mla_out_norm_g, ret_norm_g, ret_norm_b, w_out,
              ffn_norm_g, w_gate_up, w_down):
    B = x.shape[0]
    meta = jnp.broadcast_to(meta_tokens[None].astype(x.dtype), (B, N_META, D_MODEL))
    h_res = jnp.concatenate([meta, x], axis=1)
    L = h_res.shape[1]
    cos_m, sin_m = rope_tables(L, QK_ROPE)
    cos_r, sin_r = rope_tables(L, RET_HEAD)
    split_idx = [int(v) for v in np.cumsum(IN_SIZES)[:-1]]
    for l in range(DEPTH):
        h = rms_norm(h_res, attn_norm_g[l])
        z = h @ w_in[l]
        c_q, c_kv, k_pe, rq, rk, rv, rg = jnp.split(z, split_idx, axis=-1)
        y_mla = mla_mixer(c_q, c_kv, k_pe, q_a_norm_g[l], w_q_b[l], kv_a_norm_g[l], w_kv_b[l],
                          q_norm_g[l], k_norm_g[l], mla_out_norm_g[l], cos_m, sin_m)
        y_ret = retention_mixer(rq, rk, rv, rg, ret_norm_g[l], ret_norm_b[l], cos_r, sin_r)
        y = jnp.concatenate([y_mla, y_ret], axis=-1) @ w_out[l]
        h_res = h_res + y
        hf = rms_norm(h_res, ffn_norm_g[l])
        gate, up = jnp.split(hf @ w_gate_up[l], 2, axis=-1)
        h_res = h_res + (jax.nn.silu(gate) * up) @ w_down[l]
    return h_res[:, N_META:]
```

```python
import contextlib
import numpy as np
import ml_dtypes
import concourse.bass as bass
import concourse.mybir as mybir
from concourse.bass_utils import run_bass_kernel_spmd

F32 = mybir.dt.float32
BF16 = mybir.dt.bfloat16
U8 = mybir.dt.uint8
AF = mybir.ActivationFunctionType
ALU = mybir.AluOpType
AX = mybir.AxisListType

D = 1024
SEQ = 2048
NMETA = 16
NT = 17
L = NMETA + SEQ
DEPTH = 2
NIN = 2624
DFF = 2816
NFC = DFF // 128
EPS = 1e-6
QKH = 192
GAMMA = [1.0 - 2.0 ** (-5.0 - h) for h in range(4)]
LOGG = [float(np.log(np.float32(g))) for g in GAMMA]
CD = [float(np.exp(128.0 * lg)) for lg in LOGG]
ATT_SCALE = QKH ** -0.5
SCORE_REORDER = False


def tok0(t):
    return 0 if t == 0 else NMETA + 128 * (t - 1)


def ntok(t):
    return NMETA if t == 0 else 128


class Sched:
    COMPUTE = ("pe", "act", "dve", "pool")

    def __init__(self, nc, es):
        self.nc = nc
        self.es = es
        self.eng = {"pe": nc.tensor, "act": nc.scalar, "dve": nc.vector,
                    "pool": nc.gpsimd, "sp": nc.sync}
        self.prog = {e: [] for e in self.eng}
        self.sems = {}
        self.total = {}
        self.waited = {e: {} for e in self.eng}
        self.lastw = {}
        self.readers = {}
        for e in self.COMPUTE:
            self.newsem(e)

    def newsem(self, name):
        self.sems[name] = self.es.enter_context(self.nc.semaphore("s_" + name))
        self.total[name] = 0
        return name

    def _deps(self, eng, reads, writes):
        deps = {}

        def add(s, v, raw):
            if s == eng:
                if eng == "pe" or not raw:
                    return
            if v > deps.get(s, 0):
                deps[s] = v

        for r in reads:
            ev = self.lastw.get(r)
            if ev is not None:
                add(ev[0], ev[1], True)
        for w in writes:
            ev = self.lastw.get(w)
            if ev is not None:
                add(ev[0], ev[1], False)
            for s, v in self.readers.get(w, {}).items():
                add(s, v, False)
        out = []
        for s, v in deps.items():
            if self.waited[eng].get(s, 0) < v:
                self.waited[eng][s] = v
                out.append((s, v))
        return out

    def _commit(self, ev, reads, writes):
        for w in writes:
            self.lastw[w] = ev
            self.readers[w] = {}
        for r in reads:
            d = self.readers.setdefault(r, {})
            if d.get(ev[0], 0) < ev[1]:
                d[ev[0]] = ev[1]

    def op(self, eng, fn, reads=(), writes=()):
        waits = self._deps(eng, reads, writes)
        self.total[eng] += 1
        rec = _Rec()
        fn(rec)
        calls = rec.calls

        def replay(e, calls=calls):
            ins = None
            for name, a, k in calls:
                ins = getattr(e, name)(*a, **k)
            return ins
        self.prog[eng].append((waits, replay, eng, 1))
        self._commit((eng, self.total[eng]), reads, writes)

    def dma(self, q, sem, pairs, reads=(), writes=(), slow=False):
        waits = self._deps(q, reads, writes)
        for i, (o, i_) in enumerate(pairs):
            if slow:
                f = (lambda e, o=o, i_=i_: e.dma_start(out=o, in_=i_, allow_slow_non_contiguous=True))
            else:
                f = (lambda e, o=o, i_=i_: e.dma_start(out=o, in_=i_))
            self.prog[q].append((waits if i == 0 else [], f, sem, 16))
        self.total[sem] += 16 * len(pairs)
        self._commit((sem, self.total[sem]), reads, writes)

    def barrier(self):
        for e in self.eng:
            waits = []
            for s, tot in self.total.items():
                if s != e and tot > 0 and self.waited[e].get(s, 0) < tot:
                    self.waited[e][s] = tot
                    waits.append((s, tot))
            if waits:
                self.prog[e].append((waits, None, None, 0))

    def final_wait(self, q, sem):
        self.prog[q].append(([(sem, self.total[sem])], None, None, 0))

    def emit(self):
        nc = self.nc
        with nc.Block() as block:
            for name, deco in (("pe", block.tensor), ("act", block.scalar),
                               ("dve", block.vector), ("pool", block.gpsimd),
                               ("sp", block.sync)):
                prog = self.prog[name]

                def body(e, prog=prog):
                    for waits, fn, sem, inc in prog:
                        for s, v in waits:
                            e.wait_ge(self.sems[s], v)
                        if fn is not None:
                            ins = fn(e)
                            ins.then_inc(self.sems[sem], inc)
                deco(body)


class _Rec:
    def __init__(self):
        self.calls = []

    def __getattr__(self, name):
        def f(*a, **k):
            self.calls.append((name, a, k))
            return None
        return f


class Arena:
    def __init__(self, ap_u8):
        self.ap = ap_u8
        self.size = ap_u8.shape[1]
        self.off = 0

    def alloc(self, shape, dtype):
        esz = 4 if dtype == F32 else 2
        n = int(np.prod(shape))
        nbytes = n * esz
        self.off = (self.off + 63) // 64 * 64
        assert self.off + nbytes <= self.size, (self.off, nbytes, self.size)
        v = self.ap[:, self.off:self.off + nbytes].bitcast(dtype)
        self.off += nbytes
        if len(shape) == 2:
            v = v.rearrange("p (a b) -> p a b", a=shape[0])
        elif len(shape) == 3:
            v = v.rearrange("p (a b c) -> p a b c", a=shape[0], b=shape[1])
        elif len(shape) == 4:
            v = v.rearrange("p (a b c d) -> p a b c d", a=shape[0], b=shape[1], c=shape[2])
        return v

    def mark(self):
        return self.off

    def reset(self, m=0):
        self.off = m


def bc(ap, shape):
    return ap.to_broadcast(list(shape))


def build(depth=DEPTH, debug=None):
    nc = bass.Bass("TRN2", target_bir_lowering=False)
    dr = {}

    def din(name, shape, dt=F32):
        dr[name] = nc.dram_tensor(name, list(shape), dt, kind="ExternalInput").ap()
        return dr[name]

    x_d = din("x", [SEQ, D])
    meta_d = din("meta_tokens", [NMETA, D])
    attn_g_d = din("attn_norm_g", [DEPTH, D])
    w_in_d = din("w_in", [DEPTH, D, NIN])
    qag_d = din("q_a_norm_g", [DEPTH, 256])
    wqb_d = din("w_q_b", [DEPTH, 256, 768])
    kvag_d = din("kv_a_norm_g", [DEPTH, 256])
    wkvb_d = din("w_kv_b", [DEPTH, 256, 1024])
    qg_d = din("q_norm_g", [DEPTH, QKH])
    kg_d = din("k_norm_g", [DEPTH, QKH])
    og_d = din("mla_out_norm_g", [DEPTH, 512])
    rg_d = din("ret_norm_g", [DEPTH, 512])
    rb_d = din("ret_norm_b", [DEPTH, 512])
    wout_d = din("w_out", [DEPTH, D, D])
    ffng_d = din("ffn_norm_g", [DEPTH, D])
    wgu_d = din("w_gate_up", [DEPTH, D, 2 * DFF])
    wd_d = din("w_down", [DEPTH, DFF, D])
    c_ident_d = din("c_ident", [128, 128], BF16)
    c_cmask_d = din("c_cmask", [128, 128], BF16)
    c_cosR_d = din("c_cosR", [128, NT * 64])
    c_sinR_d = din("c_sinR", [128, NT * 64])
    c_cosM_d = din("c_cosM", [128, NT * 32])
    c_sinM_d = din("c_sinM", [128, NT * 32])
    c_xi_d = din("c_xi", [128, 8])
    c_zeta_d = din("c_zeta", [128, 8])
    c_dmask_d = din("c_dmask", [128, 2 * 4 * 128])
    out_d = nc.dram_tensor("out", [SEQ, D], F32, kind="ExternalOutput").ap()
    dbg_d = None
    if debug:
        dbg_d = nc.dram_tensor("dbg", [128, 4096], F32, kind="ExternalOutput").ap()

    es = contextlib.ExitStack()
    with es:
        S = Sched(nc, es)
        def toap(h):
            try:
                return h.ap()
            except Exception:
                return h[:]
        X = toap(es.enter_context(nc.sbuf_tensor("X", [128, NT, D], F32)))
        CONST_BYTES = 26 * 1024
        STAGE_BYTES = 4096
        NSTAGE = 3
        ARENA_BYTES = 229344 - 16512 - NT * D * 4 - CONST_BYTES - NSTAGE * STAGE_BYTES
        cu8 = toap(es.enter_context(nc.sbuf_tensor("CONST", [128, CONST_BYTES], U8)))
        st_t = []
        for i in range(NSTAGE):
            s_ = es.enter_context(nc.sbuf_tensor("STAGE%d" % i, [128, STAGE_BYTES // 4], F32))
            st_t.append(toap(s_))
        au8 = toap(es.enter_context(nc.sbuf_tensor("ARENA", [128, ARENA_BYTES], U8)))
        PS = toap(es.enter_context(nc.psum_tensor("PS", [128, 4096], F32)))

        def bank(b, nb=1):
            return PS[:, 512 * b:512 * (b + nb)]

        def bank16(b):
            return PS[:, 512 * b:512 * (b + 1)].bitcast(BF16)

        def pk(b):
            return ("ps", b)

        CA = Arena(cu8)
        ident = CA.alloc([128], BF16)
        cmask = CA.alloc([128], BF16)
        cosR = CA.alloc([NT, 64], F32)
        sinR = CA.alloc([NT, 64], F32)
        cosM = CA.alloc([NT, 32], F32)
        sinM = CA.alloc([NT, 32], F32)
        xi = CA.alloc([2, 4], F32)
        zeta = CA.alloc([2, 4], F32)
        dmask = CA.alloc([2, 4, 128], F32)
        QG = CA.alloc([QKH], F32)
        KG = CA.alloc([QKH], F32)
        OG = CA.alloc([512], F32)
        RG = CA.alloc([512], F32)
        RB = CA.alloc([512], F32)
        gA = CA.alloc([8], F32)
        gF = CA.alloc([8], F32)
        gQ = CA.alloc([2], F32)
        gKV = CA.alloc([2], F32)
        stat = CA.alloc([64], F32)
        mhalf = CA.alloc([8], F32)
        AR = Arena(au8)

        for nm in ("const", "gains", "stage0", "stage1", "stage2", "out", "dbg"):
            S.newsem(nm)
        xsem = [S.newsem("x%d" % t) for t in range(NT)]

        S.dma("sp", "const", [
            (ident, c_ident_d), (cmask, c_cmask_d),
            (cosR, c_cosR_d.rearrange("p (a b) -> p a b", a=NT)),
            (sinR, c_sinR_d.rearrange("p (a b) -> p a b", a=NT)),
            (cosM, c_cosM_d.rearrange("p (a b) -> p a b", a=NT)),
            (sinM, c_sinM_d.rearrange("p (a b) -> p a b", a=NT)),
            (xi, c_xi_d.rearrange("p (a b) -> p a b", a=2)),
            (zeta, c_zeta_d.rearrange("p (a b) -> p a b", a=2)),
            (dmask, c_dmask_d.rearrange("p (a b c) -> p a b c", a=2, b=4)),
        ], writes=["const"])
        def load_x(t):
            if t == 0:
                S.dma("sp", xsem[t], [(X[0:NMETA, 0, :], meta_d)], writes=[("X", 0)])
            else:
                S.dma("sp", xsem[t], [(X[:, t, :], x_d[128 * (t - 1):128 * t, :])],
                      writes=[("X", t)])
        load_x(0)
        load_x(1)

        S.op("pool", lambda e: e.memset(mhalf[:, :], -0.5), writes=["mhalf"])
        stage_i = [0]

        def load_cast(dst, src, ncols, scale=None, dst_key=None, reads=()):
            for a in range(0, ncols, 1024):
                b = min(ncols, a + 1024)
                i = stage_i[0] % NSTAGE
                stage_i[0] += 1
                st = st_t[i][:, 0:b - a]
                S.dma("sp", "stage%d" % i, [(st, src[:, a:b])], writes=[("stage", i)])
                d_ = dst[:, a:b]
                if scale is None:
                    S.op("act", lambda e, d_=d_, st=st: e.activation(out=d_, in_=st, func=AF.Copy),
                         reads=[("stage", i)] + list(reads), writes=[dst_key])
                else:
                    S.op("act", lambda e, d_=d_, st=st: e.activation(out=d_, in_=st, func=AF.Copy,
                                                                    scale=scale),
                         reads=[("stage", i)] + list(reads), writes=[dst_key])

        def load_gains(l):
            def bcast(src_row, n):
                return src_row.to_broadcast([128, n])
            S.dma("sp", "gains", [
                (QG, bcast(qg_d[l:l + 1, :], QKH)), (KG, bcast(kg_d[l:l + 1, :], QKH)),
                (OG, bcast(og_d[l:l + 1, :], 512)), (RG, bcast(rg_d[l:l + 1, :], 512)),
                (RB, bcast(rb_d[l:l + 1, :], 512)),
            ], writes=["gains"])
            S.dma("sp", "gains", [
                (gA, attn_g_d[l, :].rearrange("(k p) -> p k", p=128)),
                (gF, ffng_d[l, :].rearrange("(k p) -> p k", p=128)),
                (gQ, qag_d[l, :].rearrange("(k p) -> p k", p=128)),
                (gKV, kvag_d[l, :].rearrange("(k p) -> p k", p=128)),
            ], writes=["gcols"], slow=True)

        RSTD_LN = [False]

        def rstd_op(dst, src, inv_n, rkeys, wkey):
            n_, k_ = dst.shape
            S.op("pool", lambda e: e.tensor_scalar(out=dst, in0=src, scalar1=inv_n, scalar2=EPS,
                                                   op0=ALU.mult, op1=ALU.add),
                 reads=list(rkeys), writes=[wkey])
            S.op("pool", lambda e: e.tensor_tensor(out=dst, in0=dst, in1=mhalf[:n_, 0:k_], op=ALU.pow),
                 reads=[wkey, "mhalf"], writes=[wkey])

        def norm_T_ops(t, hn, dstT, dst_key, pb, gain=None, gain_key=None, copy_eng="dve"):
            n = ntok(t)
            ss = stat[:n, 0:1]
            rs = stat[:n, 1:2]
            S.op("act", lambda e: e.activation(out=hn[:n, :], in_=X[:n, t, :], func=AF.Square,
                                               accum_out=ss),
                 reads=[("X", t)], writes=["hn", "ss"])
            yield
            rstd_op(rs, ss, 1.0 / D, ["ss"], "rs")
            yield
            if gain is None:
                S.op("act", lambda e: e.activation(out=hn[:n, :], in_=X[:n, t, :], func=AF.Copy, scale=rs),
                     reads=[("X", t), "rs"], writes=["hn"])
            else:
                S.op("dve", lambda e: e.scalar_tensor_tensor(out=hn[:n, :], in0=X[:n, t, :], scalar=rs,
                                                             in1=gain[:n, :], op0=ALU.mult, op1=ALU.mult),
                     reads=[("X", t), "rs", gain_key], writes=["hn"])
            yield
            pT = bank16(pb)

            def tr(e):
                for kc in range(8):
                    e.transpose(out=pT[:, kc * 128:kc * 128 + n], in_=hn[:n, kc * 128:(kc + 1) * 128],
                                identity=ident[:n, :n])
            yield "pe"
            S.op("pe", tr, reads=["hn", "const"], writes=[pk(pb)])
            yield
            if copy_eng == "act":
                S.op("act", lambda e: e.activation(
                    out=dstT, in_=pT.rearrange("p (k c) -> p k c", k=8)[:, :, 0:n], func=AF.Copy),
                    reads=[pk(pb)], writes=[dst_key])
            else:
                S.op("dve", lambda e: e.tensor_copy(
                    out=dstT, in_=pT.rearrange("p (k c) -> p k c", k=8)[:, :, 0:n]),
                    reads=[pk(pb)], writes=[dst_key])
            yield

        def norm_T(t, junk, hn, dstT, dst_key, pb, gain=None, gain_key=None, copy_eng="dve"):
            for _ in norm_T_ops(t, hn, dstT, dst_key, pb, gain, gain_key, copy_eng):
                pass

        dbg_off = [0]

        def dump(ap32, key, ncols, npart=128):
            o = dbg_off[0]
            S.dma("sp", "dbg", [(dbg_d[0:npart, o:o + ncols], ap32)], reads=[key])
            dbg_off[0] += ncols

        for l in range(depth):
            load_gains(l)
            AR.reset()
            YTM = AR.alloc([4, L], BF16)
            WinM = AR.alloc([8, 576], BF16)
            Wqb = AR.alloc([2, 768], BF16)
            Wkvb = AR.alloc([2, 1024], BF16)
            KT = AR.alloc([6, L], BF16)
            V = AR.alloc([NT, 4, 130], BF16)
            junk = None
            hn = AR.alloc([1024], BF16)
            hT = AR.alloc([8, 128], BF16)
            cn = AR.alloc([512], BF16)
            cT = AR.alloc([4, 128], BF16)
            kpe = AR.alloc([64], F32)
            sq32 = AR.alloc([768], F32)
            qtmp = AR.alloc([4, QKH], F32)
            r1 = AR.alloc([256], F32)
            r2 = AR.alloc([256], F32)
            kr1 = AR.alloc([64], F32)
            kr2 = AR.alloc([64], F32)
            kr3 = AR.alloc([64], F32)
            qn = AR.alloc([4, 128], BF16)
            qp = AR.alloc([4, 64], BF16)
            kn = AR.alloc([4, 128], BF16)
            kp = AR.alloc([4, 64], BF16)
            QTb = [AR.alloc([4, 128], BF16) for _ in range(2)]
            QTzb = [AR.alloc([2, 2, 128], BF16) for _ in range(2)]
            PT = [AR.alloc([4, 128], BF16) for _ in range(2)]
            on = AR.alloc([4, 128], BF16)
            ocp = AR.alloc([516], F32)
            st8 = AR.alloc([16], F32)
            st9 = AR.alloc([8], F32)

            for kc in range(8):
                load_cast(WinM[:, kc, :], w_in_d[l, kc * 128:(kc + 1) * 128, 0:576], 576,
                          scale=gA[:, kc:kc + 1], dst_key="WinM", reads=["gcols"])
            for c in range(2):
                load_cast(Wqb[:, c, :], wqb_d[l, c * 128:(c + 1) * 128, :], 768,
                          scale=gQ[:, c:c + 1], dst_key="Wqb", reads=["gcols"])
                load_cast(Wkvb[:, c, :], wkvb_d[l, c * 128:(c + 1) * 128, :], 1024,
                          scale=gKV[:, c:c + 1], dst_key="Wkvb", reads=["gcols"])
            S.op("pool", lambda e: e.memset(V[:, :, :, 128:130], 1.0), writes=["Vones"])
            for i_ in range(2):
                S.op("pool", lambda e, i_=i_: e.memset(QTzb[i_][:, :, :, :], 0.0), writes=[("QT", i_)])
            print("pass M arena bytes used", AR.mark(), "of", AR.size)

            def front(t):
                n = ntok(t)
                c0 = tok0(t)
                QT = QTb[t % 2]
                QTz = QTzb[t % 2]
                qtk = ("QT", t % 2)
                for _ in norm_T_ops(t, hn, hT[:, :, 0:n], "hT", 2):
                    yield
                pZ = bank(0, 2)

                def mmz(e):
                    for (a, b) in ((0, 512), (512, 576)):
                        for kc in range(8):
                            e.matmul(pZ[:n, a:b], lhsT=hT[:, kc, 0:n], rhs=WinM[:, kc, a:b],
                                     start=(kc == 0), stop=(kc == 7))
                yield "pe"
                S.op("pe", mmz, reads=["hT", "WinM"], writes=[pk(0), pk(1)])
                yield
                ss2 = st8[:n, 0:2]
                rs2 = st8[:n, 2:4]
                S.op("act", lambda e: e.activation(out=sq32[:n, 0:256], in_=pZ[:n, 0:256], func=AF.Square,
                                                   accum_out=st8[:n, 0:1]),
                     reads=[pk(0)], writes=["sq32", "ssq"])
                yield
                S.op("act", lambda e: e.activation(out=sq32[:n, 256:512], in_=pZ[:n, 256:512], func=AF.Square,
                                                   accum_out=st8[:n, 1:2]),
                     reads=[pk(0)], writes=["sq32", "ssk"])
                yield
                rstd_op(rs2, ss2, 1.0 / 256, ["ssq", "ssk"], "rs2")
                yield
                S.op("act", lambda e: e.activation(out=cn[:n, 0:256], in_=pZ[:n, 0:256], func=AF.Copy,
                                                   scale=st8[:n, 2:3]),
                     reads=[pk(0), "rs2"], writes=["cn"])
                yield
                S.op("act", lambda e: e.activation(out=cn[:n, 256:512], in_=pZ[:n, 256:512], func=AF.Copy,
                                                   scale=st8[:n, 3:4]),
                     reads=[pk(0), "rs2"], writes=["cn"])
                yield
                S.op("dve", lambda e: e.tensor_copy(out=kpe[:n, :], in_=pZ[:n, 512:576]),
                     reads=[pk(1)], writes=["kpe"])
                yield
                pT = bank16(2)

                def trc(e):
                    for c in range(4):
                        e.transpose(out=pT[:, c * 128:c * 128 + n], in_=cn[:n, c * 128:(c + 1) * 128],
                                    identity=ident[:n, :n])
                yield "pe"
                S.op("pe", trc, reads=["cn", "const"], writes=[pk(2)])
                yield
                S.op("dve", lambda e: e.tensor_copy(
                    out=cT[:, :, 0:n], in_=pT[:, 0:512].rearrange("p (k c) -> p k c", k=4)[:, :, 0:n]),
                    reads=[pk(2)], writes=["cT"])
                yield
                pQ = bank(0, 2)
                pKV = bank(0, 2)

                def mmq(e):
                    for (a, b) in ((0, 512), (512, 768)):
                        for c in range(2):
                            e.matmul(pQ[:n, a:b], lhsT=cT[:, c, 0:n], rhs=Wqb[:, c, a:b],
                                     start=(c == 0), stop=(c == 1))
                yield "pe"
                S.op("pe", mmq, reads=["cT", "Wqb"], writes=[pk(0), pk(1)])
                yield
                pQ3 = pQ[:, 0:768].rearrange("p (h d) -> p h d", h=4)
                S.op("act", lambda e: e.activation(out=sq32[:n, 0:768], in_=pQ[:n, 0:768], func=AF.Square),
                     reads=[pk(0), pk(1)], writes=["sq32"])
                yield
                S.op("dve", lambda e: e.tensor_reduce(
                    out=st8[:n, 4:8], in_=sq32[:n, 0:768].rearrange("p (h d) -> p h d", h=4),
                    axis=AX.X, op=ALU.add),
                    reads=["sq32"], writes=["ssq4"])
                yield
                rstd_op(st8[:n, 4:8], st8[:n, 4:8], 1.0 / QKH, ["ssq4"], "ssq4")
                yield
                for h in range(4):
                    S.op("dve", lambda e, h=h: e.scalar_tensor_tensor(
                        out=qtmp[:n, h, :], in0=pQ3[:n, h, :], scalar=st8[:n, 4 + h:5 + h], in1=QG[:n, :],
                        op0=ALU.mult, op1=ALU.mult),
                        reads=[pk(0), pk(1), "ssq4", "gains"], writes=["qtmp"])
                    yield

                def mmkv(e):
                    for (a, b) in ((0, 512), (512, 1024)):
                        for c in range(2):
                            e.matmul(pKV[:n, a:b], lhsT=cT[:, 2 + c, 0:n], rhs=Wkvb[:, c, a:b],
                                     start=(c == 0), stop=(c == 1))
                yield "pe"
                S.op("pe", mmkv, reads=["cT", "Wkvb"], writes=[pk(0), pk(1)])
                yield
                S.op("dve", lambda e: e.tensor_copy(out=qn[:n, :, :], in_=qtmp[:n, :, 0:128]),
                     reads=["qtmp"], writes=["qn"])
                yield
                qpe4 = qtmp[:n, :, 128:192].rearrange("p h (two f) -> p h two f", two=2)
                cosb = bc(cosM[:n, t, :].unsqueeze(1).unsqueeze(1), [n, 4, 2, 32])
                sinb = bc(sinM[:n, t, :].unsqueeze(1), [n, 4, 32])
                r1v = r1[:n, :].rearrange("p (h two f) -> p h two f", h=4, two=2)
                r2v = r2[:n, :].rearrange("p (h two f) -> p h two f", h=4, two=2)
                S.op("dve", lambda e: e.tensor_tensor(out=r1v, in0=qpe4, in1=cosb, op=ALU.mult),
                     reads=["qtmp", "const"], writes=["r1"])
                yield
                S.op("dve", lambda e: e.tensor_tensor(out=r2v[:, :, 0, :], in0=qpe4[:, :, 1, :], in1=sinb,
                                                      op=ALU.mult),
                     reads=["qtmp", "const"], writes=["r2a"])
                yield
                S.op("dve", lambda e: e.tensor_tensor(out=r2v[:, :, 1, :], in0=qpe4[:, :, 0, :], in1=sinb,
                                                      op=ALU.mult),
                     reads=["qtmp", "const"], writes=["r2b"])
                yield
                qpv = qp[:n, :, :].rearrange("p h (two f) -> p h two f", two=2)
                S.op("dve", lambda e: e.tensor_tensor(out=qpv[:, :, 0, :], in0=r1v[:, :, 0, :],
                                                      in1=r2v[:, :, 0, :], op=ALU.subtract),
                     reads=["r1", "r2a"], writes=["qp"])
                yield
                S.op("dve", lambda e: e.tensor_tensor(out=qpv[:, :, 1, :], in0=r1v[:, :, 1, :],
                                                      in1=r2v[:, :, 1, :], op=ALU.add),
                     reads=["r1", "r2b"], writes=["qp"])
                yield
                def trq(e):
                    for h in range(4):
                        e.transpose(out=pT[:, h * 128:h * 128 + n], in_=qn[:n, h, :], identity=ident[:n, :n])
                    qpf = qp[:n, :, :].rearrange("p h f -> p (h f)")
                    for c in range(2):
                        e.transpose(out=pT[:, (4 + c) * 128:(4 + c) * 128 + n],
                                    in_=qpf[:, c * 128:(c + 1) * 128], identity=ident[:n, :n])
                yield "pe"
                S.op("pe", trq, reads=["qn", "qp", "const"], writes=[pk(2)])
                yield
                S.op("dve", lambda e: e.tensor_copy(
                    out=QT[:, :, 0:n], in_=pT[:, 0:512].rearrange("p (k c) -> p k c", k=4)[:, :, 0:n]),
                    reads=[pk(2)], writes=[qtk])
                yield
                pTz = pT[:, 512:768].rearrange("p (k c) -> p k c", k=2)
                S.op("dve", lambda e: e.tensor_copy(out=QTz[0:64, :, 0, 0:n], in_=pTz[0:64, :, 0:n]),
                     reads=[pk(2)], writes=[qtk])
                yield
                S.op("dve", lambda e: e.tensor_copy(out=QTz[64:128, :, 1, 0:n], in_=pTz[64:128, :, 0:n]),
                     reads=[pk(2)], writes=[qtk])
                yield
                pKV3 = pKV.rearrange("p (h d) -> p h d", h=4)
                S.op("act", lambda e: e.activation(
                    out=sq32[:n, 0:512].rearrange("p (h d) -> p h d", h=4), in_=pKV3[:n, :, 0:128],
                    func=AF.Square),
                    reads=[pk(0), pk(1)], writes=["sq32"])
                yield
                S.op("dve", lambda e: e.tensor_reduce(
                    out=st8[:n, 8:12], in_=sq32[:n, 0:512].rearrange("p (h d) -> p h d", h=4),
                    axis=AX.X, op=ALU.add),
                    reads=["sq32"], writes=["ssk4"])
                yield
                S.op("act", lambda e: e.activation(out=kr1[:n, :], in_=kpe[:n, :], func=AF.Square,
                                                   accum_out=st8[:n, 12:13]),
                     reads=["kpe"], writes=["kr1", "sskpe"])
                yield
                S.op("dve", lambda e: e.tensor_scalar(out=st8[:n, 8:12], in0=st8[:n, 8:12],
                                                      scalar1=st8[:n, 12:13], scalar2=None, op0=ALU.add),
                     reads=["ssk4", "sskpe"], writes=["ssk4"])
                yield
                rstd_op(st8[:n, 8:12], st8[:n, 8:12], 1.0 / QKH, ["ssk4"], "ssk4")
                yield
                for h in range(4):
                    S.op("dve", lambda e, h=h: e.scalar_tensor_tensor(
                        out=kn[:n, h, :], in0=pKV3[:n, h, 0:128], scalar=st8[:n, 8 + h:9 + h],
                        in1=KG[:n, 0:128], op0=ALU.mult, op1=ALU.mult),
                        reads=[pk(0), pk(1), "ssk4", "gains"], writes=["kn"])
                    yield
                S.op("act", lambda e: e.activation(out=V[:n, t, :, 0:128], in_=pKV3[:n, :, 128:256],
                                                   func=AF.Copy),
                     reads=[pk(0), pk(1)], writes=[("V", t)])
                yield
                S.op("dve", lambda e: e.tensor_tensor(out=kr1[:n, :], in0=kpe[:n, :], in1=KG[:n, 128:192],
                                                      op=ALU.mult),
                     reads=["kpe", "gains", "sskpe"], writes=["kr1"])
                yield
                k1v = kr1[:n, :].rearrange("p (two f) -> p two f", two=2)
                k2v = kr2[:n, :].rearrange("p (two f) -> p two f", two=2)
                k3v = kr3[:n, :].rearrange("p (two f) -> p two f", two=2)
                cosb2 = bc(cosM[:n, t, :].unsqueeze(1), [n, 2, 32])
                S.op("dve", lambda e: e.tensor_tensor(out=k2v, in0=k1v, in1=cosb2, op=ALU.mult),
                     reads=["kr1", "const"], writes=["kr2"])
                yield
                S.op("dve", lambda e: e.tensor_tensor(out=k3v[:, 0, :], in0=k1v[:, 1, :], in1=sinM[:n, t, :],
                                                      op=ALU.mult),
                     reads=["kr1", "const"], writes=["kr3a"])
                yield
                S.op("dve", lambda e: e.tensor_tensor(out=k3v[:, 1, :], in0=k1v[:, 0, :], in1=sinM[:n, t, :],
                                                      op=ALU.mult),
                     reads=["kr1", "const"], writes=["kr3b"])
                yield
                S.op("dve", lambda e: e.tensor_tensor(out=k2v[:, 0, :], in0=k2v[:, 0, :], in1=k3v[:, 0, :],
                                                      op=ALU.subtract),
                     reads=["kr2", "kr3a"], writes=["kr2"])
                yield
                S.op("dve", lambda e: e.tensor_tensor(out=k2v[:, 1, :], in0=k2v[:, 1, :], in1=k3v[:, 1, :],
                                                      op=ALU.add),
                     reads=["kr2", "kr3b"], writes=["kr2"])
                yield
                S.op("dve", lambda e: e.tensor_tensor(
                    out=kp[:n, :, :], in0=bc(kr2[:n, :].unsqueeze(1), [n, 4, 64]),
                    in1=bc(st8[:n, 8:12].unsqueeze(2), [n, 4, 64]), op=ALU.mult),
                    reads=["kr2", "ssk4"], writes=["kp"])
                yield

                def trk(e):
                    for h in range(4):
                        e.transpose(out=pT[:, h * 128:h * 128 + n], in_=kn[:n, h, :], identity=ident[:n, :n])
                    kpf = kp[:n, :, :].rearrange("p h f -> p (h f)")
                    for c in range(2):
                        e.transpose(out=pT[:, (4 + c) * 128:(4 + c) * 128 + n],
                                    in_=kpf[:, c * 128:(c + 1) * 128], identity=ident[:n, :n])
                yield "pe"
                S.op("pe", trk, reads=["kn", "kp", "const"], writes=[pk(2)])
                yield
                S.op("act", lambda e: e.activation(
                    out=KT[:, :, c0:c0 + n], in_=pT[:, 0:768].rearrange("p (k c) -> p k c", k=6)[:, :, 0:n],
                    func=AF.Copy),
                    reads=[pk(2)], writes=[("KT", t)])
                yield

            def attention(t, gen):
                n = ntok(t)
                c0 = tok0(t)
                QT = QTb[t % 2]
                QTz = QTzb[t % 2]
                qtk = ("QT", t % 2)
                spj = -(-152 // (t + 1))

                budget = [0]

                def adv(k, force=False):
                    budget[0] += k
                    while budget[0] > 0:
                        try:
                            r = next(gen)
                        except StopIteration:
                            return
                        if r == "pe":
                            if not force:
                                return
                            continue
                        budget[0] -= 1

                def scores(j):
                    m = ntok(j)
                    k0 = tok0(j)
                    sb = 3 + (j % 2)
                    pS3 = bank(sb).rearrange("p (h c) -> p h c", h=4)

                    def mms(e):
                        for h in range(4):
                            e.matmul(pS3[:m, h, 0:n], lhsT=KT[:, h, k0:k0 + m], rhs=QT[:, h, 0:n],
                                     start=(h == 0), stop=False, skip_group_check=True)
                        for hp in range(2):
                            e.matmul(pS3[:m, 2 * hp:2 * hp + 2, 0:n], lhsT=KT[:, 4 + hp, k0:k0 + m],
                                     rhs=QTz[:, hp, :, 0:n], start=False, stop=True, skip_group_check=True)
                    S.op("pe", mms, reads=[("KT", j), qtk], writes=[pk(sb)])

                def obank(h):
                    return bank(5 + h // 2)[:, (h % 2) * 129:(h % 2) * 129 + 129]

                scores(0)
                for j in range(t + 1):
                    m = ntok(j)
                    sb = 3 + (j % 2)
                    pS3 = bank(sb).rearrange("p (h c) -> p h c", h=4)
                    pt = PT[j % 2]
                    ptk = ("PT", j % 2)
                    if j + 1 <= t:
                        scores(j + 1)
                    S.op("act", lambda e: e.activation(out=pt[:m, :, 0:n], in_=pS3[:m, :, 0:n],
                                                       func=AF.Exp, scale=ATT_SCALE),
                         reads=[pk(sb)], writes=[ptk])
                    if j == t:
                        S.op("dve", lambda e: e.tensor_tensor(
                            out=pt[:m, :, 0:n], in0=pt[:m, :, 0:n],
                            in1=bc(cmask[:m, 0:n].unsqueeze(1), [m, 4, n]), op=ALU.mult),
                            reads=[ptk, "const"], writes=[ptk])

                    def mmpv(e):
                        for h in range(4):
                            e.matmul(obank(h)[:n, :], lhsT=pt[:m, h, 0:n], rhs=V[:m, j, h, 0:129],
                                     start=(j == 0 and h % 2 == 0), stop=(j == t), skip_group_check=True)
                    S.op("pe", mmpv, reads=[ptk, ("V", j), "Vones"], writes=[pk(5), pk(6)])
                    adv(spj)
                    if j == 0:
                        while pending_tro:
                            pending_tro.pop(0)()
                ocp3 = ocp[:n, :].rearrange("p (h c) -> p h c", h=4)
                S.op("dve", lambda e: e.tensor_copy(out=ocp[:n, 0:258], in_=bank(5)[:n, 0:258]),
                     reads=[pk(5)], writes=["ocp"])
                S.op("dve", lambda e: e.tensor_copy(out=ocp[:n, 258:516], in_=bank(6)[:n, 0:258]),
                     reads=[pk(6)], writes=["ocp"])
                adv(2)
                S.op("dve", lambda e: e.reciprocal(out=st9[:n, 0:4], in_=ocp3[:, :, 128]),
                     reads=["ocp"], writes=["rden"])
                S.op("dve", lambda e: e.tensor_tensor(out=ocp3[:, :, 0:128], in0=ocp3[:, :, 0:128],
                                                      in1=bc(st9[:n, 0:4].unsqueeze(2), [n, 4, 128]), op=ALU.mult),
                     reads=["ocp", "rden"], writes=["ocp"])
                adv(2)
                for h in range(4):
                    S.op("act", lambda e: e.activation(out=on[:n, h, :], in_=ocp3[:, h, 0:128],
                                                       func=AF.Square, accum_out=st9[:n, 4 + h:5 + h]),
                         reads=["ocp"], writes=["on", "oss"])
                adv(2)
                rstd_op(st9[:n, 4:8], st9[:n, 4:8], 1.0 / 128, ["oss"], "oss")
                S.op("dve", lambda e: e.tensor_tensor(out=ocp3[:, :, 0:128], in0=ocp3[:, :, 0:128],
                                                      in1=bc(st9[:n, 4:8].unsqueeze(2), [n, 4, 128]), op=ALU.mult),
                     reads=["ocp", "oss"], writes=["ocp"])
                S.op("dve", lambda e: e.tensor_tensor(out=on[:n, :, :], in0=ocp3[:, :, 0:128],
                                                      in1=OG[:n, :].rearrange("p (h c) -> p h c", h=4),
                                                      op=ALU.mult),
                     reads=["ocp", "gains"], writes=["on"])
                adv(2)
                def fin(t=t, n=n, c0=c0):
                    pT7 = bank16(7)

                    def tro(e):
                        for h in range(4):
                            e.transpose(out=pT7[:, h * 128:h * 128 + n], in_=on[:n, h, :], identity=ident[:n, :n])
                    S.op("pe", tro, reads=["on", "const"], writes=[pk(7)])
                    S.op("act", lambda e: e.activation(
                        out=YTM[:, :, c0:c0 + n],
                        in_=pT7[:, 0:512].rearrange("p (k c) -> p k c", k=4)[:, :, 0:n], func=AF.Copy),
                        reads=[pk(7)], writes=[("YTM", t)])
                pending_tro.append(fin)
                adv(10000, force=True)

            pending_tro = []
            if l == 0:
                for t_ in range(2, NT):
                    load_x(t_)
            for _ in front(0):
                pass
            for t in range(NT):
                attention(t, front(t + 1) if t + 1 < NT else iter(()))
            while pending_tro:
                pending_tro.pop(0)()
            RSTD_LN[0] = False
            S.barrier()
            AR.reset()
            YTM2 = AR.alloc([4, L], BF16)
            WinR = AR.alloc([8, 2048], BF16)
            Wout = AR.alloc([8, 1024], BF16)
            junk = None
            hn = AR.alloc([1024], BF16)
            hTb = [AR.alloc([8, 128], BF16) for _ in range(2)]
            qa = AR.alloc([512], F32)
            qb = AR.alloc([512], F32)
            ka = AR.alloc([512], F32)
            kb = AR.alloc([512], F32)
            qtil = AR.alloc([4, 128], BF16)
            krr = AR.alloc([4, 128], BF16)
            khat = AR.alloc([4, 128], BF16)
            vbf = AR.alloc([4, 128], BF16)
            gs = AR.alloc([512], F32)
            qkT = AR.alloc([8, 128], BF16)
            PTr = AR.alloc([4, 128], BF16)
            S32 = AR.alloc([4, 128], F32)
            Sbf = AR.alloc([4, 128], BF16)
            osq = AR.alloc([512], F32)
            onr = AR.alloc([512], F32)
            yb = AR.alloc([512], BF16)
            yT = AR.alloc([4, 128], BF16)
            st8 = AR.alloc([32], F32)

            for kc in range(8):
                load_cast(WinR[:, kc, :], w_in_d[l, kc * 128:(kc + 1) * 128, 576:2624], 2048,
                          scale=gA[:, kc:kc + 1], dst_key="WinR", reads=["gcols"])
            for kc in range(8):
                load_cast(Wout[:, kc, :], wout_d[l, kc * 128:(kc + 1) * 128, :], 1024,
                          dst_key="Wout")
            S.op("pool", lambda e: e.memset(S32[:, :, :], 0.0), writes=["S32"])
            S.op("pool", lambda e: e.memset(Sbf[:, :, :], 0.0), writes=["Sbf"])

            pending_xadd = []
            for t in range(NT):
                n = ntok(t)
                c0 = tok0(t)
                sel = 1 if t == 0 else 0
                hT = hTb[t % 2]
                if t == 0:
                    norm_T(0, junk, hn, hT[:, :, 0:n], ("hT", 0), 4, copy_eng="act")
                pZ = bank(0, 4)

                def emit_z(tt):
                    nn = ntok(tt)
                    hTt = hTb[tt % 2]

                    def mmz(e):
                        for cc in range(4):
                            for kc in range(8):
                                e.matmul(pZ[:nn, cc * 512:(cc + 1) * 512], lhsT=hTt[:, kc, 0:nn],
                                         rhs=WinR[:, kc, cc * 512:(cc + 1) * 512],
                                         start=(kc == 0), stop=(kc == 7))
                    S.op("pe", mmz, reads=[("hT", tt % 2), "WinR"], writes=[pk(0), pk(1), pk(2), pk(3)])
                if t == 0:
                    emit_z(0)
                cosb = bc(cosR[:n, t, :].unsqueeze(1).unsqueeze(1), [n, 4, 2, 64])
                sinb = bc(sinR[:n, t, :].unsqueeze(1), [n, 4, 64])

                def rope(src_bank_key, src, ta, tb, nm):
                    x4 = src.rearrange("p (h two f) -> p h two f", h=4, two=2)
                    a4 = ta[:n, :].rearrange("p (h two f) -> p h two f", h=4, two=2)
                    b4 = tb[:n, :].rearrange("p (h two f) -> p h two f", h=4, two=2)
                    S.op("dve", lambda e: e.tensor_tensor(out=a4, in0=x4, in1=cosb, op=ALU.mult),
                         reads=[src_bank_key, "const"], writes=[nm + "a"])
                    S.op("dve", lambda e: e.tensor_tensor(out=b4[:, :, 0, :], in0=x4[:, :, 1, :], in1=sinb,
                                                          op=ALU.mult),
                         reads=[src_bank_key, "const"], writes=[nm + "b0"])
                    S.op("dve", lambda e: e.tensor_tensor(out=b4[:, :, 1, :], in0=x4[:, :, 0, :], in1=sinb,
                                                          op=ALU.mult),
                         reads=[src_bank_key, "const"], writes=[nm + "b1"])
                    S.op("dve", lambda e: e.tensor_tensor(out=a4[:, :, 0, :], in0=a4[:, :, 0, :],
                                                           in1=b4[:, :, 0, :], op=ALU.subtract),
                         reads=[nm + "a", nm + "b0"], writes=[nm + "a"])
                    S.op("dve", lambda e: e.tensor_tensor(out=a4[:, :, 1, :], in0=a4[:, :, 1, :],
                                                           in1=b4[:, :, 1, :], op=ALU.add),
                         reads=[nm + "a", nm + "b1"], writes=[nm + "a"])
                rope(pk(0), pZ[:n, 0:512], qa, qb, "rq")
                rope(pk(1), pZ[:n, 512:1024], ka, kb, "rk")
                qa3 = qa[:n, :].rearrange("p (h d) -> p h d", h=4)
                ka3 = ka[:n, :].rearrange("p (h d) -> p h d", h=4)
                S.op("dve", lambda e: e.tensor_tensor(out=qtil[:n, :, :], in0=qa3,
                                                       in1=bc(xi[:n, sel, :].unsqueeze(2), [n, 4, 128]),
                                                       op=ALU.mult),
                     reads=["rqa", "const"], writes=["qtil"])
                S.op("dve", lambda e: e.tensor_scalar(out=krr[:n, :, :], in0=ka3, scalar1=128.0 ** -0.5,
                                                       scalar2=None, op0=ALU.mult),
                     reads=["rka"], writes=["krr"])
                S.op("dve", lambda e: e.tensor_tensor(out=khat[:n, :, :], in0=ka3,
                                                       in1=bc(zeta[:n, sel, :].unsqueeze(2), [n, 4, 128]),
                                                       op=ALU.mult),
                     reads=["rka", "const"], writes=["khat"])
                while pending_xadd:
                    pending_xadd.pop(0)()
                S.op("act", lambda e: e.activation(out=vbf[:n, :, :].rearrange("p h d -> p (h d)"),
                                                   in_=pZ[:n, 1024:1536], func=AF.Copy),
                     reads=[pk(2)], writes=["vbf"])
                S.op("act", lambda e: e.activation(out=gs[:n, :], in_=pZ[:n, 1536:2048], func=AF.Silu),
                     reads=[pk(3)], writes=["gs"])
                if t + 1 < NT:
                    n1 = ntok(t + 1)
                    norm_T(t + 1, junk, hn, hTb[(t + 1) % 2][:, :, 0:n1], ("hT", (t + 1) % 2), 4,
                           copy_eng="act")
                pT = bank16(4)

                def trqk(e):
                    ins = None
                    for h in range(4):
                        ins = e.transpose(out=pT[:, h * 128:h * 128 + n], in_=qtil[:n, h, :],
                                          identity=ident[:n, :n])
                    for h in range(4):
                        ins = e.transpose(out=pT[:, (4 + h) * 128:(4 + h) * 128 + n], in_=krr[:n, h, :],
                                          identity=ident[:n, :n])
                    return ins
                S.op("pe", trqk, reads=["qtil", "krr", "const"], writes=[pk(4)])
                S.op("act", lambda e: e.activation(
                    out=qkT[:, :, 0:n], in_=pT.rearrange("p (k c) -> p k c", k=8)[:, :, 0:n], func=AF.Copy),
                    reads=[pk(4)], writes=["qkT"])
                pRS = bank(5).rearrange("p (h c) -> p h c", h=4)
                pSU = bank(6).rearrange("p (h c) -> p h c", h=4)
                pRO = bank(7).rearrange("p (h c) -> p h c", h=4)

                def mmrs(e):
                    ins = None
                    for h in range(4):
                        ins = e.matmul(pRS[:n, h, 0:n], lhsT=qkT[:, 4 + h, 0:n], rhs=qkT[:, h, 0:n],
                                       start=True, stop=True)
                    return ins
                S.op("pe", mmrs, reads=["qkT"], writes=[pk(5)])
                S.op("dve", lambda e: e.tensor_tensor(out=PTr[:n, :, 0:n], in0=pRS[:n, :, 0:n],
                                                      in1=dmask[:n, sel, :, 0:n], op=ALU.mult),
                     reads=[pk(5), "const"], writes=["PTr"])

                def mmro(e):
                    ins = None
                    for h in range(4):
                        e.matmul(pRO[:n, h, :], lhsT=PTr[:n, h, 0:n], rhs=vbf[:n, h, :], start=True, stop=False)
                        ins = e.matmul(pRO[:n, h, :], lhsT=qkT[:, h, 0:n], rhs=Sbf[:, h, :],
                                       start=False, stop=True)
                    return ins
                S.op("pe", mmro, reads=["PTr", "vbf", "qkT", "Sbf"], writes=[pk(7)])

                def mmsu(e):
                    ins = None
                    for h in range(4):
                        ins = e.matmul(pSU[:, h, :], lhsT=khat[:n, h, :], rhs=vbf[:n, h, :],
                                       start=True, stop=True)
                    return ins
                S.op("pe", mmsu, reads=["khat", "vbf"], writes=[pk(6)])
                if t + 1 < NT:
                    emit_z(t + 1)
                for h in range(4):
                    S.op("dve", lambda e, h=h: e.scalar_tensor_tensor(
                        out=S32[:, h, :], in0=S32[:, h, :], scalar=CD[h], in1=pSU[:, h, :],
                        op0=ALU.mult, op1=ALU.add),
                        reads=[pk(6), "S32"], writes=["S32"])
                S.op("pool", lambda e: e.tensor_copy(out=Sbf[:, :, :], in_=S32[:, :, :]),
                     reads=["S32"], writes=["Sbf"])
                S.op("dve", lambda e: e.tensor_reduce(out=st8[:n, 0:4], in_=pRO[:n, :, :], axis=AX.X,
                                                      op=ALU.add),
                     reads=[pk(7)], writes=["gsum"])
                S.op("act", lambda e: e.activation(out=osq[:n, :].rearrange("p (h d) -> p h d", h=4),
                                                   in_=pRO[:n, :, :], func=AF.Square),
                     reads=[pk(7)], writes=["osq"])
                S.op("dve", lambda e: e.tensor_reduce(
                    out=st8[:n, 4:8], in_=osq[:n, :].rearrange("p (h d) -> p h d", h=4), axis=AX.X,
                    op=ALU.add),
                    reads=["osq"], writes=["gssq"])
                S.op("dve", lambda e: e.tensor_scalar(out=st8[:n, 0:4], in0=st8[:n, 0:4], scalar1=1.0 / 128,
                                                      scalar2=None, op0=ALU.mult),
                     reads=["gsum"], writes=["gsum"])
                S.op("dve", lambda e: e.tensor_tensor(out=st8[:n, 8:12], in0=st8[:n, 0:4], in1=st8[:n, 0:4],
                                                      op=ALU.mult),
                     reads=["gsum"], writes=["gmsq"])
                S.op("dve", lambda e: e.scalar_tensor_tensor(out=st8[:n, 4:8], in0=st8[:n, 4:8],
                                                             scalar=1.0 / 128, in1=st8[:n, 8:12],
                                                             op0=ALU.mult, op1=ALU.subtract),
                     reads=["gssq", "gmsq"], writes=["gssq"])
                rstd_op(st8[:n, 4:8], st8[:n, 4:8], 1.0, ["gssq"], "gssq")
                S.op("dve", lambda e: e.scalar_tensor_tensor(out=st8[:n, 12:16], in0=st8[:n, 0:4],
                                                             scalar=-1.0, in1=st8[:n, 4:8],
                                                             op0=ALU.mult, op1=ALU.mult),
                     reads=["gsum", "gssq"], writes=["gnb"])
                for h in range(4):
                    S.op("dve", lambda e, h=h: e.tensor_scalar(
                        out=onr[:n, h * 128:(h + 1) * 128], in0=pRO[:n, h, :], scalar1=st8[:n, 4 + h:5 + h],
                        scalar2=st8[:n, 12 + h:13 + h], op0=ALU.mult, op1=ALU.add),
                        reads=[pk(7), "gssq", "gnb"], writes=["onr"])
                S.op("dve", lambda e: e.tensor_tensor(out=onr[:n, :], in0=onr[:n, :], in1=RG[:n, :],
                                                       op=ALU.mult),
                     reads=["onr", "gains"], writes=["onr"])
                S.op("dve", lambda e: e.tensor_tensor(out=onr[:n, :], in0=onr[:n, :], in1=RB[:n, :],
                                                       op=ALU.add),
                     reads=["onr", "gains"], writes=["onr"])
                S.op("dve", lambda e: e.tensor_tensor(out=yb[:n, :], in0=onr[:n, :], in1=gs[:n, :],
                                                       op=ALU.mult),
                     reads=["onr", "gs"], writes=["yb"])
                pT = bank16(4)

                def try_(e):
                    ins = None
                    for h in range(4):
                        ins = e.transpose(out=pT[:, h * 128:h * 128 + n], in_=yb[:n, h * 128:(h + 1) * 128],
                                          identity=ident[:n, :n])
                    return ins
                S.op("pe", try_, reads=["yb", "const"], writes=[pk(4)])
                S.op("act", lambda e: e.activation(
                    out=yT[:, :, 0:n], in_=pT[:, 0:512].rearrange("p (k c) -> p k c", k=4)[:, :, 0:n],
                    func=AF.Copy),
                    reads=[pk(4)], writes=["yT"])
                pW = bank(5, 2)

                def mmw(e):
                    ins = None
                    for hf in range(2):
                        for kc in range(8):
                            lt = YTM2[:, kc, c0:c0 + n] if kc < 4 else yT[:, kc - 4, 0:n]
                            ins = e.matmul(pW[:n, hf * 512:(hf + 1) * 512], lhsT=lt,
                                           rhs=Wout[:, kc, hf * 512:(hf + 1) * 512],
                                           start=(kc == 0), stop=(kc == 7))
                    return ins
                S.op("pe", mmw, reads=[("YTM", t), "yT", "Wout"], writes=[pk(5), pk(6)])
                def xadd(t=t, n=n, pW=pW):
                    S.op("dve", lambda e: e.tensor_tensor(out=X[:n, t, :], in0=X[:n, t, :], in1=pW[:n, :],
                                                          op=ALU.add),
                         reads=[("X", t), pk(5), pk(6)], writes=[("X", t)])
                pending_xadd.append(xadd)
            while pending_xadd:
                pending_xadd.pop(0)()
            if debug == "R":
                break
            S.barrier()
            AR.reset()
            hfT = AR.alloc([8, L], BF16)
            NSLOT = 8
            WGU = [AR.alloc([8, 256], BF16) for _ in range(NSLOT)]
            WD = [AR.alloc([1024], BF16) for _ in range(NSLOT)]
            junk = None
            hn = AR.alloc([1024], BF16)
            sg = [AR.alloc([512], F32) for _ in range(2)]
            actT = [AR.alloc([4, 512], BF16) for _ in range(2)]
            GFb = AR.alloc([1024], F32)
            S.dma("sp", "gains", [(GFb, ffng_d[l:l + 1, :].to_broadcast([128, D]))], writes=["GFb"])
            chunks = [[0]] + [list(range(1 + 4 * c, 5 + 4 * c)) for c in range(4)]

            def norm_chunk(cidx):
                if cidx >= len(chunks):
                    return
                for t_ in chunks[cidx]:
                    n_ = ntok(t_)
                    c0_ = tok0(t_)
                    norm_T(t_, junk, hn, hfT[:, :, c0_:c0_ + n_], ("hfT", t_), 4 + (t_ % 2), gain=GFb,
                           gain_key="GFb")
            norm_chunk(0)
            groups = [list(range(g, min(g + 4, NFC))) for g in range(0, NFC, 4)]
            ci = 0
            di = 0
            for g, grp in enumerate(groups):
                for j in grp:
                    sl = j % NSLOT
                    wsrc = wgu_d[l].rearrange("(k p) (two f) -> p k two f", p=128, two=2)
                    for two in range(2):
                        i = stage_i[0] % NSTAGE
                        stage_i[0] += 1
                        st = st_t[i]
                        stv = st[:, 0:1024].rearrange("p (k f) -> p k f", k=8)
                        S.dma("sp", "stage%d" % i, [(stv, wsrc[:, :, two, j * 128:(j + 1) * 128])],
                              writes=[("stage", i)])
                        S.op("act", lambda e, sl=sl, stv=stv, two=two: e.activation(
                            out=WGU[sl][:, :, two * 128:(two + 1) * 128], in_=stv, func=AF.Copy),
                            reads=[("stage", i)], writes=[("WGU", sl)])
                    load_cast(WD[sl][:, :], wd_d[l, j * 128:(j + 1) * 128, :], 1024, dst_key=("WD", sl))
                for chi, ch in enumerate(chunks):
                    cs = tok0(ch[0])
                    ncol = sum(ntok(t) for t in ch)
                    ab = actT[ci % 2]
                    abk = ("actT", ci % 2)
                    ci += 1
                    for jl, j in enumerate(grp):
                        sl = j % NSLOT
                        pb = 2 * (di % 2)
                        sgi = sg[di % 2]
                        sgk = ("sg", di % 2)
                        di += 1
                        pG = bank(pb)
                        pU = bank(pb + 1)
                        genN = None
                        if g == 0 and chi + 1 < len(chunks) and jl < len(chunks[chi + 1]):
                            t_ = chunks[chi + 1][jl]
                            n_ = ntok(t_)
                            c0_ = tok0(t_)
                            genN = norm_T_ops(t_, hn, hfT[:, :, c0_:c0_ + n_], ("hfT", t_), 4 + (t_ % 2),
                                              gain=GFb, gain_key="GFb")
                            for _ in range(3):
                                next(genN)

                        def mmgu(e, sl=sl, pG=pG, pU=pU):
                            ins = None
                            for kc in range(8):
                                ins = e.matmul(pG[:, 0:ncol], lhsT=WGU[sl][:, kc, 0:128],
                                               rhs=hfT[:, kc, cs:cs + ncol], start=(kc == 0), stop=(kc == 7))
                            for kc in range(8):
                                ins = e.matmul(pU[:, 0:ncol], lhsT=WGU[sl][:, kc, 128:256],
                                               rhs=hfT[:, kc, cs:cs + ncol], start=(kc == 0), stop=(kc == 7))
                            return ins
                        S.op("pe", mmgu, reads=[("WGU", sl)] + [("hfT", t) for t in ch],
                             writes=[pk(pb), pk(pb + 1)])
                        S.op("act", lambda e, pG=pG, sgi=sgi: e.activation(out=sgi[:, 0:ncol], in_=pG[:, 0:ncol],
                                                                          func=AF.Silu),
                             reads=[pk(pb)], writes=[sgk])
                        S.op("dve", lambda e, pU=pU, sgi=sgi, ab=ab, jl=jl: e.tensor_tensor(
                            out=ab[:, jl, 0:ncol], in0=sgi[:, 0:ncol], in1=pU[:, 0:ncol], op=ALU.mult),
                            reads=[sgk, pk(pb + 1)], writes=[abk])
                        if genN is not None:
                            for _ in genN:
                                pass
                    for t in ch:
                        n = ntok(t)
                        o0 = tok0(t) - cs
                        for hf in range(2):
                            pb = 4 + (hf + 2 * (t % 2))
                            pD = bank(pb)

                            def mmd(e, pD=pD, o0=o0, n=n, hf=hf, ab=ab):
                                ins = None
                                for jl, j in enumerate(grp):
                                    ins = e.matmul(pD[:n, :], lhsT=ab[:, jl, o0:o0 + n],
                                                   rhs=WD[j % NSLOT][:, hf * 512:(hf + 1) * 512],
                                                   start=(jl == 0), stop=(jl == len(grp) - 1))
                                return ins
                            S.op("pe", mmd, reads=[abk] + [("WD", j % NSLOT) for j in grp], writes=[pk(pb)])
                            S.op("dve", lambda e, pD=pD, n=n, t=t, hf=hf: e.tensor_tensor(
                                out=X[:n, t, hf * 512:(hf + 1) * 512], in0=X[:n, t, hf * 512:(hf + 1) * 512],
                                in1=pD[:n, :], op=ALU.add),
                                reads=[("X", t), pk(pb)], writes=[("X", t)])
            S.barrier()

        if debug:
            S.final_wait("sp", "dbg")
        for t in range(1, NT):
            S.dma("sp", "out", [(out_d[128 * (t - 1):128 * t, :], X[:, t, :])], reads=[("X", t)])
        S.final_wait("sp", "out")
        S.emit()
    return nc


def host_consts():
    pos = np.zeros((128, NT), dtype=np.float32)
    local = np.zeros((128, 2), dtype=np.float64)
    for t in range(NT):
        for p in range(128):
            pos[p, t] = tok0(t) + p if p < ntok(t) else 0
    p_idx = np.arange(128)
    local[:, 0] = p_idx
    local[:, 1] = np.minimum(112 + p_idx, 127)

    def tables(dim):
        inv = (10000.0 ** (-(np.arange(0, dim, 2, dtype=np.float32)) / dim)).astype(np.float32)
        ang = (pos[:, :, None] * inv[None, None, :]).astype(np.float32)
        return (np.cos(ang.astype(np.float64)).astype(np.float32).reshape(128, -1),
                np.sin(ang.astype(np.float64)).astype(np.float32).reshape(128, -1))
    cosR, sinR = tables(128)
    cosM, sinM = tables(64)
    lg = np.array([np.log(np.float32(g)) for g in GAMMA], dtype=np.float64)
    xi = np.zeros((128, 2, 4)); zeta = np.zeros((128, 2, 4)); dm = np.zeros((128, 2, 4, 128))
    nn = np.arange(128)
    for s in range(2):
        for h in range(4):
            xi[:, s, h] = np.exp((local[:, s] + 1.0) * lg[h])
            zeta[:, s, h] = np.exp((127.0 - local[:, s]) * lg[h]) * (128.0 ** -0.5)
            dm[:, s, h, :] = np.exp(-(local[:, s] + 1.0) * lg[h])[:, None] * (nn[None, :] >= nn[:, None])
    cm = (nn[None, :] >= nn[:, None]).astype(np.float32)
    return {
        "c_ident": np.eye(128, dtype=np.float32).astype(ml_dtypes.bfloat16),
        "c_cmask": cm.astype(ml_dtypes.bfloat16),
        "c_cosR": cosR, "c_sinR": sinR, "c_cosM": cosM, "c_sinM": sinM,
        "c_xi": xi.reshape(128, 8).astype(np.float32),
        "c_zeta": zeta.reshape(128, 8).astype(np.float32),
        "c_dmask": dm.reshape(128, -1).astype(np.float32),
    }


_NC_CACHE = {}


def kernel(**inputs):
    ins = {k: np.ascontiguousarray(np.asarray(v)) for k, v in inputs.items()}
    if "nc" not in _NC_CACHE:
        _NC_CACHE["nc"] = build()
    nc = _NC_CACHE["nc"]
    consts = host_consts()
    in_maps = []
    for b in range(8):
        m = {k: v for k, v in ins.items() if k != "x"}
        m["x"] = np.ascontiguousarray(ins["x"][b])
        m.update(consts)
        in_maps.append(m)
    res = run_bass_kernel_spmd(nc, in_maps, core_ids=list(range(8)))
    out = np.stack([np.asarray(r["out"]) for r in res.results], axis=0)
    return out.astype(np.float32)
```

```python
import contextlib
import numpy as np
import ml_dtypes
import concourse.bass as bass
import concourse.mybir as mybir
from concourse.bass_utils import run_bass_kernel_spmd

F32 = mybir.dt.float32
BF16 = mybir.dt.bfloat16
U8 = mybir.dt.uint8
AF = mybir.ActivationFunctionType
ALU = mybir.AluOpType
AX = mybir.AxisListType

D = 1024
SEQ = 2048
NMETA = 16
NT = 17
L = NMETA + SEQ
DEPTH = 2
NIN = 2624
DFF = 2816
NFC = DFF // 128
EPS = 1e-6
QKH = 192
GAMMA = [1.0 - 2.0 ** (-5.0 - h) for h in range(4)]
LOGG = [float(np.log(np.float32(g))) for g in GAMMA]
CD = [float(np.exp(128.0 * lg)) for lg in LOGG]
ATT_SCALE = QKH ** -0.5
SCORE_REORDER = False


def tok0(t):
    return 0 if t == 0 else NMETA + 128 * (t - 1)


def ntok(t):
    return NMETA if t == 0 else 128


class Sched:
    COMPUTE = ("pe", "act", "dve", "pool")

    def __init__(self, nc, es):
        self.nc = nc
        self.es = es
        self.eng = {"pe": nc.tensor, "act": nc.scalar, "dve": nc.vector,
                    "pool": nc.gpsimd, "sp": nc.sync}
        self.prog = {e: [] for e in self.eng}
        self.sems = {}
        self.total = {}
        self.waited = {e: {} for e in self.eng}
        self.lastw = {}
        self.readers = {}
        for e in self.COMPUTE:
            self.newsem(e)

    def newsem(self, name):
        self.sems[name] = self.es.enter_context(self.nc.semaphore("s_" + name))
        self.total[name] = 0
        return name

    def _deps(self, eng, reads, writes):
        deps = {}

        def add(s, v, raw):
            if s == eng:
                if eng == "pe" or not raw:
                    return
            if v > deps.get(s, 0):
                deps[s] = v

        for r in reads:
            ev = self.lastw.get(r)
            if ev is not None:
                add(ev[0], ev[1], True)
        for w in writes:
            ev = self.lastw.get(w)
            if ev is not None:
                add(ev[0], ev[1], False)
            for s, v in self.readers.get(w, {}).items():
                add(s, v, False)
        out = []
        for s, v in deps.items():
            if self.waited[eng].get(s, 0) < v:
                self.waited[eng][s] = v
                out.append((s, v))
        return out

    def _commit(self, ev, reads, writes):
        for w in writes:
            self.lastw[w] = ev
            self.readers[w] = {}
        for r in reads:
            d = self.readers.setdefault(r, {})
            if d.get(ev[0], 0) < ev[1]:
                d[ev[0]] = ev[1]

    def op(self, eng, fn, reads=(), writes=()):
        waits = self._deps(eng, reads, writes)
        self.total[eng] += 1
        rec = _Rec()
        fn(rec)
        calls = rec.calls

        def replay(e, calls=calls):
            ins = None
            for name, a, k in calls:
                ins = getattr(e, name)(*a, **k)
            return ins
        self.prog[eng].append((waits, replay, eng, 1))
        self._commit((eng, self.total[eng]), reads, writes)

    def dma(self, q, sem, pairs, reads=(), writes=(), slow=False):
        waits = self._deps(q, reads, writes)
        for i, (o, i_) in enumerate(pairs):
            if slow:
                f = (lambda e, o=o, i_=i_: e.dma_start(out=o, in_=i_, allow_slow_non_contiguous=True))
            else:
                f = (lambda e, o=o, i_=i_: e.dma_start(out=o, in_=i_))
            self.prog[q].append((waits if i == 0 else [], f, sem, 16))
        self.total[sem] += 16 * len(pairs)
        self._commit((sem, self.total[sem]), reads, writes)

    def barrier(self):
        for e in self.eng:
            waits = []
            for s, tot in self.total.items():
                if s != e and tot > 0 and self.waited[e].get(s, 0) < tot:
                    self.waited[e][s] = tot
                    waits.append((s, tot))
            if waits:
                self.prog[e].append((waits, None, None, 0))

    def final_wait(self, q, sem):
        self.prog[q].append(([(sem, self.total[sem])], None, None, 0))

    def emit(self):
        nc = self.nc
        with nc.Block() as block:
            for name, deco in (("pe", block.tensor), ("act", block.scalar),
                               ("dve", block.vector), ("pool", block.gpsimd),
                               ("sp", block.sync)):
                prog = self.prog[name]

                def body(e, prog=prog):
                    for waits, fn, sem, inc in prog:
                        for s, v in waits:
                            e.wait_ge(self.sems[s], v)
                        if fn is not None:
                            ins = fn(e)
                            ins.then_inc(self.sems[sem], inc)
                deco(body)


class _Rec:
    def __init__(self):
        self.calls = []

    def __getattr__(self, name):
        def f(*a, **k):
            self.calls.append((name, a, k))
            return None
        return f


class Arena:
    def __init__(self, ap_u8):
        self.ap = ap_u8
        self.size = ap_u8.shape[1]
        self.off = 0

    def alloc(self, shape, dtype):
        esz = 4 if dtype == F32 else 2
        n = int(np.prod(shape))
        nbytes = n * esz
        self.off = (self.off + 63) // 64 * 64
        assert self.off + nbytes <= self.size, (self.off, nbytes, self.size)
        v = self.ap[:, self.off:self.off + nbytes].bitcast(dtype)
        self.off += nbytes
        if len(shape) == 2:
            v = v.rearrange("p (a b) -> p a b", a=shape[0])
        elif len(shape) == 3:
            v = v.rearrange("p (a b c) -> p a b c", a=shape[0], b=shape[1])
        elif len(shape) == 4:
            v = v.rearrange("p (a b c d) -> p a b c d", a=shape[0], b=shape[1], c=shape[2])
        return v

    def mark(self):
        return self.off

    def reset(self, m=0):
        self.off = m


def bc(ap, shape):
    return ap.to_broadcast(list(shape))


def build(depth=DEPTH, debug=None):
    nc = bass.Bass("TRN2", target_bir_lowering=False)
    dr = {}

    def din(name, shape, dt=F32):
        dr[name] = nc.dram_tensor(name, list(shape), dt, kind="ExternalInput").ap()
        return dr[name]

    x_d = din("x", [SEQ, D])
    meta_d = din("meta_tokens", [NMETA, D])
    attn_g_d = din("attn_norm_g", [DEPTH, D])
    w_in_d = din("w_in", [DEPTH, D, NIN])
    qag_d = din("q_a_norm_g", [DEPTH, 256])
    wqb_d = din("w_q_b", [DEPTH, 256, 768])
    kvag_d = din("kv_a_norm_g", [DEPTH, 256])
    wkvb_d = din("w_kv_b", [DEPTH, 256, 1024])
    qg_d = din("q_norm_g", [DEPTH, QKH])
    kg_d = din("k_norm_g", [DEPTH, QKH])
    og_d = din("mla_out_norm_g", [DEPTH, 512])
    rg_d = din("ret_norm_g", [DEPTH, 512])
    rb_d = din("ret_norm_b", [DEPTH, 512])
    wout_d = din("w_out", [DEPTH, D, D])
    ffng_d = din("ffn_norm_g", [DEPTH, D])
    wgu_d = din("w_gate_up", [DEPTH, D, 2 * DFF])
    wd_d = din("w_down", [DEPTH, DFF, D])
    c_ident_d = din("c_ident", [128, 128], BF16)
    c_cmask_d = din("c_cmask", [128, 128], BF16)
    c_cosR_d = din("c_cosR", [128, NT * 64])
    c_sinR_d = din("c_sinR", [128, NT * 64])
    c_cosM_d = din("c_cosM", [128, NT * 32])
    c_sinM_d = din("c_sinM", [128, NT * 32])
    c_xi_d = din("c_xi", [128, 8])
    c_zeta_d = din("c_zeta", [128, 8])
    c_dmask_d = din("c_dmask", [128, 2 * 4 * 128])
    out_d = nc.dram_tensor("out", [SEQ, D], F32, kind="ExternalOutput").ap()
    dbg_d = None
    if debug:
        dbg_d = nc.dram_tensor("dbg", [128, 4096], F32, kind="ExternalOutput").ap()

    es = contextlib.ExitStack()
    with es:
        S = Sched(nc, es)
        def toap(h):
            try:
                return h.ap()
            except Exception:
                return h[:]
        X = toap(es.enter_context(nc.sbuf_tensor("X", [128, NT, D], F32)))
        CONST_BYTES = 26 * 1024
        STAGE_BYTES = 4096
        NSTAGE = 3
        ARENA_BYTES = 229344 - 16512 - NT * D * 4 - CONST_BYTES - NSTAGE * STAGE_BYTES
        cu8 = toap(es.enter_context(nc.sbuf_tensor("CONST", [128, CONST_BYTES], U8)))
        st_t = []
        for i in range(NSTAGE):
            s_ = es.enter_context(nc.sbuf_tensor("STAGE%d" % i, [128, STAGE_BYTES // 4], F32))
            st_t.append(toap(s_))
        au8 = toap(es.enter_context(nc.sbuf_tensor("ARENA", [128, ARENA_BYTES], U8)))
        PS = toap(es.enter_context(nc.psum_tensor("PS", [128, 4096], F32)))

        def bank(b, nb=1):
            return PS[:, 512 * b:512 * (b + nb)]

        def bank16(b):
            return PS[:, 512 * b:512 * (b + 1)].bitcast(BF16)

        def pk(b):
            return ("ps", b)

        CA = Arena(cu8)
        ident = CA.alloc([128], BF16)
        cmask = CA.alloc([128], BF16)
        cosR = CA.alloc([NT, 64], F32)
        sinR = CA.alloc([NT, 64], F32)
        cosM = CA.alloc([NT, 32], F32)
        sinM = CA.alloc([NT, 32], F32)
        xi = CA.alloc([2, 4], F32)
        zeta = CA.alloc([2, 4], F32)
        dmask = CA.alloc([2, 4, 128], F32)
        QG = CA.alloc([QKH], F32)
        KG = CA.alloc([QKH], F32)
        OG = CA.alloc([512], F32)
        RG = CA.alloc([512], F32)
        RB = CA.alloc([512], F32)
        gA = CA.alloc([8], F32)
        gF = CA.alloc([8], F32)
        gQ = CA.alloc([2], F32)
        gKV = CA.alloc([2], F32)
        stat = CA.alloc([64], F32)
        mhalf = CA.alloc([8], F32)
        AR = Arena(au8)

        for nm in ("const", "gains", "stage0", "stage1", "stage2", "out", "dbg"):
            S.newsem(nm)
        xsem = [S.newsem("x%d" % t) for t in range(NT)]

        S.dma("sp", "const", [
            (ident, c_ident_d), (cmask, c_cmask_d),
            (cosR, c_cosR_d.rearrange("p (a b) -> p a b", a=NT)),
            (sinR, c_sinR_d.rearrange("p (a b) -> p a b", a=NT)),
            (cosM, c_cosM_d.rearrange("p (a b) -> p a b", a=NT)),
            (sinM, c_sinM_d.rearrange("p (a b) -> p a b", a=NT)),
            (xi, c_xi_d.rearrange("p (a b) -> p a b", a=2)),
            (zeta, c_zeta_d.rearrange("p (a b) -> p a b", a=2)),
            (dmask, c_dmask_d.rearrange("p (a b c) -> p a b c", a=2, b=4)),
        ], writes=["const"])
        for t in range(NT):
            if t == 0:
                S.dma("sp", xsem[t], [(X[0:NMETA, 0, :], meta_d)], writes=[("X", 0)])
            else:
                S.dma("sp", xsem[t], [(X[:, t, :], x_d[128 * (t - 1):128 * t, :])],
                      writes=[("X", t)])

        S.op("pool", lambda e: e.memset(mhalf[:, :], -0.5), writes=["mhalf"])
        stage_i = [0]

        def load_cast(dst, src, ncols, scale=None, dst_key=None, reads=()):
            for a in range(0, ncols, 1024):
                b = min(ncols, a + 1024)
                i = stage_i[0] % NSTAGE
                stage_i[0] += 1
                st = st_t[i][:, 0:b - a]
                S.dma("sp", "stage%d" % i, [(st, src[:, a:b])], writes=[("stage", i)])
                d_ = dst[:, a:b]
                if scale is None:
                    S.op("act", lambda e, d_=d_, st=st: e.activation(out=d_, in_=st, func=AF.Copy),
                         reads=[("stage", i)] + list(reads), writes=[dst_key])
                else:
                    S.op("act", lambda e, d_=d_, st=st: e.activation(out=d_, in_=st, func=AF.Copy,
                                                                    scale=scale),
                         reads=[("stage", i)] + list(reads), writes=[dst_key])

        def load_gains(l):
            def bcast(src_row, n):
                return src_row.to_broadcast([128, n])
            S.dma("sp", "gains", [
                (QG, bcast(qg_d[l:l + 1, :], QKH)), (KG, bcast(kg_d[l:l + 1, :], QKH)),
                (OG, bcast(og_d[l:l + 1, :], 512)), (RG, bcast(rg_d[l:l + 1, :], 512)),
                (RB, bcast(rb_d[l:l + 1, :], 512)),
            ], writes=["gains"])
            S.dma("sp", "gains", [
                (gA, attn_g_d[l, :].rearrange("(k p) -> p k", p=128)),
                (gF, ffng_d[l, :].rearrange("(k p) -> p k", p=128)),
                (gQ, qag_d[l, :].rearrange("(k p) -> p k", p=128)),
                (gKV, kvag_d[l, :].rearrange("(k p) -> p k", p=128)),
            ], writes=["gcols"], slow=True)

        RSTD_LN = [False]

        def rstd_op(dst, src, inv_n, rkeys, wkey):
            n_, k_ = dst.shape
            S.op("pool", lambda e: e.tensor_scalar(out=dst, in0=src, scalar1=inv_n, scalar2=EPS,
                                                   op0=ALU.mult, op1=ALU.add),
                 reads=list(rkeys), writes=[wkey])
            S.op("pool", lambda e: e.tensor_tensor(out=dst, in0=dst, in1=mhalf[:n_, 0:k_], op=ALU.pow),
                 reads=[wkey, "mhalf"], writes=[wkey])

        def norm_T_ops(t, hn, dstT, dst_key, pb, gain=None, gain_key=None):
            n = ntok(t)
            ss = stat[:n, 0:1]
            rs = stat[:n, 1:2]
            S.op("act", lambda e: e.activation(out=hn[:n, :], in_=X[:n, t, :], func=AF.Square,
                                               accum_out=ss),
                 reads=[("X", t)], writes=["hn", "ss"])
            yield
            rstd_op(rs, ss, 1.0 / D, ["ss"], "rs")
            yield
            if gain is None:
                S.op("act", lambda e: e.activation(out=hn[:n, :], in_=X[:n, t, :], func=AF.Copy, scale=rs),
                     reads=[("X", t), "rs"], writes=["hn"])
            else:
                S.op("dve", lambda e: e.scalar_tensor_tensor(out=hn[:n, :], in0=X[:n, t, :], scalar=rs,
                                                             in1=gain[:n, :], op0=ALU.mult, op1=ALU.mult),
                     reads=[("X", t), "rs", gain_key], writes=["hn"])
            yield
            pT = bank16(pb)

            def tr(e):
                for kc in range(8):
                    e.transpose(out=pT[:, kc * 128:kc * 128 + n], in_=hn[:n, kc * 128:(kc + 1) * 128],
                                identity=ident[:n, :n])
            yield "pe"
            S.op("pe", tr, reads=["hn", "const"], writes=[pk(pb)])
            yield
            S.op("dve", lambda e: e.tensor_copy(
                out=dstT, in_=pT.rearrange("p (k c) -> p k c", k=8)[:, :, 0:n]),
                reads=[pk(pb)], writes=[dst_key])
            yield

        def norm_T(t, junk, hn, dstT, dst_key, pb, gain=None, gain_key=None):
            for _ in norm_T_ops(t, hn, dstT, dst_key, pb, gain, gain_key):
                pass

        dbg_off = [0]

        def dump(ap32, key, ncols, npart=128):
            o = dbg_off[0]
            S.dma("sp", "dbg", [(dbg_d[0:npart, o:o + ncols], ap32)], reads=[key])
            dbg_off[0] += ncols

        stored = set()
        for l in range(depth):
            load_gains(l)
            AR.reset()
            YTM = AR.alloc([4, L], BF16)
            WinM = AR.alloc([8, 576], BF16)
            Wqb = AR.alloc([2, 768], BF16)
            Wkvb = AR.alloc([2, 1024], BF16)
            KT = AR.alloc([6, L], BF16)
            V = AR.alloc([NT, 4, 130], BF16)
            junk = None
            hn = AR.alloc([1024], BF16)
            hT = AR.alloc([8, 128], BF16)
            cn = AR.alloc([512], BF16)
            cT = AR.alloc([4, 128], BF16)
            kpe = AR.alloc([64], F32)
            sq32 = AR.alloc([768], F32)
            qtmp = AR.alloc([4, QKH], F32)
            r1 = AR.alloc([256], F32)
            r2 = AR.alloc([256], F32)
            kr1 = AR.alloc([64], F32)
            kr2 = AR.alloc([64], F32)
            kr3 = AR.alloc([64], F32)
            qn = AR.alloc([4, 128], BF16)
            qp = AR.alloc([4, 64], BF16)
            kn = AR.alloc([4, 128], BF16)
            kp = AR.alloc([4, 64], BF16)
            QTb = [AR.alloc([4, 128], BF16) for _ in range(2)]
            QTzb = [AR.alloc([2, 2, 128], BF16) for _ in range(2)]
            PT = [AR.alloc([4, 128], BF16) for _ in range(2)]
            on = AR.alloc([4, 128], BF16)
            ocp = AR.alloc([516], F32)
            st8 = AR.alloc([16], F32)
            st9 = AR.alloc([8], F32)

            for kc in range(8):
                load_cast(WinM[:, kc, :], w_in_d[l, kc * 128:(kc + 1) * 128, 0:576], 576,
                          scale=gA[:, kc:kc + 1], dst_key="WinM", reads=["gcols"])
            for c in range(2):
                load_cast(Wqb[:, c, :], wqb_d[l, c * 128:(c + 1) * 128, :], 768,
                          scale=gQ[:, c:c + 1], dst_key="Wqb", reads=["gcols"])
                load_cast(Wkvb[:, c, :], wkvb_d[l, c * 128:(c + 1) * 128, :], 1024,
                          scale=gKV[:, c:c + 1], dst_key="Wkvb", reads=["gcols"])
            S.op("pool", lambda e: e.memset(V[:, :, :, 128:130], 1.0), writes=["Vones"])
            for i_ in range(2):
                S.op("pool", lambda e, i_=i_: e.memset(QTzb[i_][:, :, :, :], 0.0), writes=[("QT", i_)])
            print("pass M arena bytes used", AR.mark(), "of", AR.size)

            def front(t):
                n = ntok(t)
                c0 = tok0(t)
                QT = QTb[t % 2]
                QTz = QTzb[t % 2]
                qtk = ("QT", t % 2)
                for _ in norm_T_ops(t, hn, hT[:, :, 0:n], "hT", 2):
                    yield
                pZ = bank(0, 2)

                def mmz(e):
                    for (a, b) in ((0, 512), (512, 576)):
                        for kc in range(8):
                            e.matmul(pZ[:n, a:b], lhsT=hT[:, kc, 0:n], rhs=WinM[:, kc, a:b],
                                     start=(kc == 0), stop=(kc == 7))
                yield "pe"
                S.op("pe", mmz, reads=["hT", "WinM"], writes=[pk(0), pk(1)])
                yield
                ss2 = st8[:n, 0:2]
                rs2 = st8[:n, 2:4]
                S.op("act", lambda e: e.activation(out=sq32[:n, 0:256], in_=pZ[:n, 0:256], func=AF.Square,
                                                   accum_out=st8[:n, 0:1]),
                     reads=[pk(0)], writes=["sq32", "ssq"])
                yield
                S.op("act", lambda e: e.activation(out=sq32[:n, 256:512], in_=pZ[:n, 256:512], func=AF.Square,
                                                   accum_out=st8[:n, 1:2]),
                     reads=[pk(0)], writes=["sq32", "ssk"])
                yield
                rstd_op(rs2, ss2, 1.0 / 256, ["ssq", "ssk"], "rs2")
                yield
                S.op("act", lambda e: e.activation(out=cn[:n, 0:256], in_=pZ[:n, 0:256], func=AF.Copy,
                                                   scale=st8[:n, 2:3]),
                     reads=[pk(0), "rs2"], writes=["cn"])
                yield
                S.op("act", lambda e: e.activation(out=cn[:n, 256:512], in_=pZ[:n, 256:512], func=AF.Copy,
                                                   scale=st8[:n, 3:4]),
                     reads=[pk(0), "rs2"], writes=["cn"])
                yield
                S.op("dve", lambda e: e.tensor_copy(out=kpe[:n, :], in_=pZ[:n, 512:576]),
                     reads=[pk(1)], writes=["kpe"])
                yield
                pT = bank16(2)

                def trc(e):
                    for c in range(4):
                        e.transpose(out=pT[:, c * 128:c * 128 + n], in_=cn[:n, c * 128:(c + 1) * 128],
                                    identity=ident[:n, :n])
                yield "pe"
                S.op("pe", trc, reads=["cn", "const"], writes=[pk(2)])
                yield
                S.op("dve", lambda e: e.tensor_copy(
                    out=cT[:, :, 0:n], in_=pT[:, 0:512].rearrange("p (k c) -> p k c", k=4)[:, :, 0:n]),
                    reads=[pk(2)], writes=["cT"])
                yield
                pQ = bank(0, 2)
                pKV = bank(0, 2)

                def mmq(e):
                    for (a, b) in ((0, 512), (512, 768)):
                        for c in range(2):
                            e.matmul(pQ[:n, a:b], lhsT=cT[:, c, 0:n], rhs=Wqb[:, c, a:b],
                                     start=(c == 0), stop=(c == 1))
                yield "pe"
                S.op("pe", mmq, reads=["cT", "Wqb"], writes=[pk(0), pk(1)])
                yield
                pQ3 = pQ[:, 0:768].rearrange("p (h d) -> p h d", h=4)
                S.op("act", lambda e: e.activation(out=sq32[:n, 0:768], in_=pQ[:n, 0:768], func=AF.Square),
                     reads=[pk(0), pk(1)], writes=["sq32"])
                yield
                S.op("dve", lambda e: e.tensor_reduce(
                    out=st8[:n, 4:8], in_=sq32[:n, 0:768].rearrange("p (h d) -> p h d", h=4),
                    axis=AX.X, op=ALU.add),
                    reads=["sq32"], writes=["ssq4"])
                yield
                rstd_op(st8[:n, 4:8], st8[:n, 4:8], 1.0 / QKH, ["ssq4"], "ssq4")
                yield
                for h in range(4):
                    S.op("dve", lambda e, h=h: e.scalar_tensor_tensor(
                        out=qtmp[:n, h, :], in0=pQ3[:n, h, :], scalar=st8[:n, 4 + h:5 + h], in1=QG[:n, :],
                        op0=ALU.mult, op1=ALU.mult),
                        reads=[pk(0), pk(1), "ssq4", "gains"], writes=["qtmp"])
                    yield

                def mmkv(e):
                    for (a, b) in ((0, 512), (512, 1024)):
                        for c in range(2):
                            e.matmul(pKV[:n, a:b], lhsT=cT[:, 2 + c, 0:n], rhs=Wkvb[:, c, a:b],
                                     start=(c == 0), stop=(c == 1))
                yield "pe"
                S.op("pe", mmkv, reads=["cT", "Wkvb"], writes=[pk(0), pk(1)])
                yield
                S.op("dve", lambda e: e.tensor_copy(out=qn[:n, :, :], in_=qtmp[:n, :, 0:128]),
                     reads=["qtmp"], writes=["qn"])
                yield
                qpe4 = qtmp[:n, :, 128:192].rearrange("p h (two f) -> p h two f", two=2)
                cosb = bc(cosM[:n, t, :].unsqueeze(1).unsqueeze(1), [n, 4, 2, 32])
                sinb = bc(sinM[:n, t, :].unsqueeze(1), [n, 4, 32])
                r1v = r1[:n, :].rearrange("p (h two f) -> p h two f", h=4, two=2)
                r2v = r2[:n, :].rearrange("p (h two f) -> p h two f", h=4, two=2)
                S.op("dve", lambda e: e.tensor_tensor(out=r1v, in0=qpe4, in1=cosb, op=ALU.mult),
                     reads=["qtmp", "const"], writes=["r1"])
                yield
                S.op("dve", lambda e: e.tensor_tensor(out=r2v[:, :, 0, :], in0=qpe4[:, :, 1, :], in1=sinb,
                                                      op=ALU.mult),
                     reads=["qtmp", "const"], writes=["r2a"])
                yield
                S.op("dve", lambda e: e.tensor_tensor(out=r2v[:, :, 1, :], in0=qpe4[:, :, 0, :], in1=sinb,
                                                      op=ALU.mult),
                     reads=["qtmp", "const"], writes=["r2b"])
                yield
                qpv = qp[:n, :, :].rearrange("p h (two f) -> p h two f", two=2)
                S.op("dve", lambda e: e.tensor_tensor(out=qpv[:, :, 0, :], in0=r1v[:, :, 0, :],
                                                      in1=r2v[:, :, 0, :], op=ALU.subtract),
                     reads=["r1", "r2a"], writes=["qp"])
                yield
                S.op("dve", lambda e: e.tensor_tensor(out=qpv[:, :, 1, :], in0=r1v[:, :, 1, :],
                                                      in1=r2v[:, :, 1, :], op=ALU.add),
                     reads=["r1", "r2b"], writes=["qp"])
                yield
                def trq(e):
                    for h in range(4):
                        e.transpose(out=pT[:, h * 128:h * 128 + n], in_=qn[:n, h, :], identity=ident[:n, :n])
                    qpf = qp[:n, :, :].rearrange("p h f -> p (h f)")
                    for c in range(2):
                        e.transpose(out=pT[:, (4 + c) * 128:(4 + c) * 128 + n],
                                    in_=qpf[:, c * 128:(c + 1) * 128], identity=ident[:n, :n])
                yield "pe"
                S.op("pe", trq, reads=["qn", "qp", "const"], writes=[pk(2)])
                yield
                S.op("dve", lambda e: e.tensor_copy(
                    out=QT[:, :, 0:n], in_=pT[:, 0:512].rearrange("p (k c) -> p k c", k=4)[:, :, 0:n]),
                    reads=[pk(2)], writes=[qtk])
                yield
                pTz = pT[:, 512:768].rearrange("p (k c) -> p k c", k=2)
                S.op("dve", lambda e: e.tensor_copy(out=QTz[0:64, :, 0, 0:n], in_=pTz[0:64, :, 0:n]),
                     reads=[pk(2)], writes=[qtk])
                yield
                S.op("dve", lambda e: e.tensor_copy(out=QTz[64:128, :, 1, 0:n], in_=pTz[64:128, :, 0:n]),
                     reads=[pk(2)], writes=[qtk])
                yield
                pKV3 = pKV.rearrange("p (h d) -> p h d", h=4)
                S.op("act", lambda e: e.activation(
                    out=sq32[:n, 0:512].rearrange("p (h d) -> p h d", h=4), in_=pKV3[:n, :, 0:128],
                    func=AF.Square),
                    reads=[pk(0), pk(1)], writes=["sq32"])
                yield
                S.op("dve", lambda e: e.tensor_reduce(
                    out=st8[:n, 8:12], in_=sq32[:n, 0:512].rearrange("p (h d) -> p h d", h=4),
                    axis=AX.X, op=ALU.add),
                    reads=["sq32"], writes=["ssk4"])
                yield
                S.op("act", lambda e: e.activation(out=kr1[:n, :], in_=kpe[:n, :], func=AF.Square,
                                                   accum_out=st8[:n, 12:13]),
                     reads=["kpe"], writes=["kr1", "sskpe"])
                yield
                S.op("dve", lambda e: e.tensor_scalar(out=st8[:n, 8:12], in0=st8[:n, 8:12],
                                                      scalar1=st8[:n, 12:13], scalar2=None, op0=ALU.add),
                     reads=["ssk4", "sskpe"], writes=["ssk4"])
                yield
                rstd_op(st8[:n, 8:12], st8[:n, 8:12], 1.0 / QKH, ["ssk4"], "ssk4")
                yield
                for h in range(4):
                    S.op("dve", lambda e, h=h: e.scalar_tensor_tensor(
                        out=kn[:n, h, :], in0=pKV3[:n, h, 0:128], scalar=st8[:n, 8 + h:9 + h],
                        in1=KG[:n, 0:128], op0=ALU.mult, op1=ALU.mult),
                        reads=[pk(0), pk(1), "ssk4", "gains"], writes=["kn"])
                    yield
                S.op("act", lambda e: e.activation(out=V[:n, t, :, 0:128], in_=pKV3[:n, :, 128:256],
                                                   func=AF.Copy),
                     reads=[pk(0), pk(1)], writes=[("V", t)])
                yield
                S.op("dve", lambda e: e.tensor_tensor(out=kr1[:n, :], in0=kpe[:n, :], in1=KG[:n, 128:192],
                                                      op=ALU.mult),
                     reads=["kpe", "gains", "sskpe"], writes=["kr1"])
                yield
                k1v = kr1[:n, :].rearrange("p (two f) -> p two f", two=2)
                k2v = kr2[:n, :].rearrange("p (two f) -> p two f", two=2)
                k3v = kr3[:n, :].rearrange("p (two f) -> p two f", two=2)
                cosb2 = bc(cosM[:n, t, :].unsqueeze(1), [n, 2, 32])
                S.op("dve", lambda e: e.tensor_tensor(out=k2v, in0=k1v, in1=cosb2, op=ALU.mult),
                     reads=["kr1", "const"], writes=["kr2"])
                yield
                S.op("dve", lambda e: e.tensor_tensor(out=k3v[:, 0, :], in0=k1v[:, 1, :], in1=sinM[:n, t, :],
                                                      op=ALU.mult),
                     reads=["kr1", "const"], writes=["kr3a"])
                yield
                S.op("dve", lambda e: e.tensor_tensor(out=k3v[:, 1, :], in0=k1v[:, 0, :], in1=sinM[:n, t, :],
                                                      op=ALU.mult),
                     reads=["kr1", "const"], writes=["kr3b"])
                yield
                S.op("dve", lambda e: e.tensor_tensor(out=k2v[:, 0, :], in0=k2v[:, 0, :], in1=k3v[:, 0, :],
                                                      op=ALU.subtract),
                     reads=["kr2", "kr3a"], writes=["kr2"])
                yield
                S.op("dve", lambda e: e.tensor_tensor(out=k2v[:, 1, :], in0=k2v[:, 1, :], in1=k3v[:, 1, :],
                                                      op=ALU.add),
                     reads=["kr2", "kr3b"], writes=["kr2"])
                yield
                S.op("dve", lambda e: e.tensor_tensor(
                    out=kp[:n, :, :], in0=bc(kr2[:n, :].unsqueeze(1), [n, 4, 64]),
                    in1=bc(st8[:n, 8:12].unsqueeze(2), [n, 4, 64]), op=ALU.mult),
                    reads=["kr2", "ssk4"], writes=["kp"])
                yield

                def trk(e):
                    for h in range(4):
                        e.transpose(out=pT[:, h * 128:h * 128 + n], in_=kn[:n, h, :], identity=ident[:n, :n])
                    kpf = kp[:n, :, :].rearrange("p h f -> p (h f)")
                    for c in range(2):
                        e.transpose(out=pT[:, (4 + c) * 128:(4 + c) * 128 + n],
                                    in_=kpf[:, c * 128:(c + 1) * 128], identity=ident[:n, :n])
                yield "pe"
                S.op("pe", trk, reads=["kn", "kp", "const"], writes=[pk(2)])
                yield
                S.op("act", lambda e: e.activation(
                    out=KT[:, :, c0:c0 + n], in_=pT[:, 0:768].rearrange("p (k c) -> p k c", k=6)[:, :, 0:n],
                    func=AF.Copy),
                    reads=[pk(2)], writes=[("KT", t)])
                yield

            def attention(t, gen):
                n = ntok(t)
                c0 = tok0(t)
                QT = QTb[t % 2]
                QTz = QTzb[t % 2]
                qtk = ("QT", t % 2)
                spj = -(-76 // (t + 1))

                budget = [0]

                def adv(k, force=False):
                    budget[0] += k
                    while budget[0] > 0:
                        try:
                            r = next(gen)
                        except StopIteration:
                            return
                        if r == "pe":
                            if not force:
                                return
                            continue
                        budget[0] -= 1

                def scores(j):
                    m = ntok(j)
                    k0 = tok0(j)
                    sb = 3 + (j % 2)
                    pS3 = bank(sb).rearrange("p (h c) -> p h c", h=4)

                    def mms(e):
                        for h in range(4):
                            e.matmul(pS3[:m, h, 0:n], lhsT=KT[:, h, k0:k0 + m], rhs=QT[:, h, 0:n],
                                     start=(h == 0), stop=False, skip_group_check=True)
                        for hp in range(2):
                            e.matmul(pS3[:m, 2 * hp:2 * hp + 2, 0:n], lhsT=KT[:, 4 + hp, k0:k0 + m],
                                     rhs=QTz[:, hp, :, 0:n], start=False, stop=True, skip_group_check=True)
                    S.op("pe", mms, reads=[("KT", j), qtk], writes=[pk(sb)])

                def obank(h):
                    return bank(5 + h // 2)[:, (h % 2) * 129:(h % 2) * 129 + 129]

                scores(0)
                for j in range(t + 1):
                    m = ntok(j)
                    sb = 3 + (j % 2)
                    pS3 = bank(sb).rearrange("p (h c) -> p h c", h=4)
                    pt = PT[j % 2]
                    ptk = ("PT", j % 2)
                    if j + 1 <= t:
                        scores(j + 1)
                    S.op("act", lambda e: e.activation(out=pt[:m, :, 0:n], in_=pS3[:m, :, 0:n],
                                                       func=AF.Exp, scale=ATT_SCALE),
                         reads=[pk(sb)], writes=[ptk])
                    if j == t:
                        S.op("dve", lambda e: e.tensor_tensor(
                            out=pt[:m, :, 0:n], in0=pt[:m, :, 0:n],
                            in1=bc(cmask[:m, 0:n].unsqueeze(1), [m, 4, n]), op=ALU.mult),
                            reads=[ptk, "const"], writes=[ptk])

                    def mmpv(e):
                        for h in range(4):
                            e.matmul(obank(h)[:n, :], lhsT=pt[:m, h, 0:n], rhs=V[:m, j, h, 0:129],
                                     start=(j == 0 and h % 2 == 0), stop=(j == t), skip_group_check=True)
                    S.op("pe", mmpv, reads=[ptk, ("V", j), "Vones"], writes=[pk(5), pk(6)])
                    adv(spj)
                    if j == 0:
                        while pending_tro:
                            pending_tro.pop(0)()
                ocp3 = ocp[:n, :].rearrange("p (h c) -> p h c", h=4)
                S.op("dve", lambda e: e.tensor_copy(out=ocp[:n, 0:258], in_=bank(5)[:n, 0:258]),
                     reads=[pk(5)], writes=["ocp"])
                S.op("dve", lambda e: e.tensor_copy(out=ocp[:n, 258:516], in_=bank(6)[:n, 0:258]),
                     reads=[pk(6)], writes=["ocp"])
                adv(2)
                S.op("dve", lambda e: e.reciprocal(out=st9[:n, 0:4], in_=ocp3[:, :, 128]),
                     reads=["ocp"], writes=["rden"])
                S.op("dve", lambda e: e.tensor_tensor(out=ocp3[:, :, 0:128], in0=ocp3[:, :, 0:128],
                                                      in1=bc(st9[:n, 0:4].unsqueeze(2), [n, 4, 128]), op=ALU.mult),
                     reads=["ocp", "rden"], writes=["ocp"])
                adv(2)
                for h in range(4):
                    S.op("act", lambda e: e.activation(out=on[:n, h, :], in_=ocp3[:, h, 0:128],
                                                       func=AF.Square, accum_out=st9[:n, 4 + h:5 + h]),
                         reads=["ocp"], writes=["on", "oss"])
                adv(2)
                rstd_op(st9[:n, 4:8], st9[:n, 4:8], 1.0 / 128, ["oss"], "oss")
                S.op("dve", lambda e: e.tensor_tensor(out=ocp3[:, :, 0:128], in0=ocp3[:, :, 0:128],
                                                      in1=bc(st9[:n, 4:8].unsqueeze(2), [n, 4, 128]), op=ALU.mult),
                     reads=["ocp", "oss"], writes=["ocp"])
                S.op("dve", lambda e: e.tensor_tensor(out=on[:n, :, :], in0=ocp3[:, :, 0:128],
                                                      in1=OG[:n, :].rearrange("p (h c) -> p h c", h=4),
                                                      op=ALU.mult),
                     reads=["ocp", "gains"], writes=["on"])
                adv(2)
                def fin(t=t, n=n, c0=c0):
                    pT7 = bank16(7)

                    def tro(e):
                        for h in range(4):
                            e.transpose(out=pT7[:, h * 128:h * 128 + n], in_=on[:n, h, :], identity=ident[:n, :n])
                    S.op("pe", tro, reads=["on", "const"], writes=[pk(7)])
                    S.op("act", lambda e: e.activation(
                        out=YTM[:, :, c0:c0 + n],
                        in_=pT7[:, 0:512].rearrange("p (k c) -> p k c", k=4)[:, :, 0:n], func=AF.Copy),
                        reads=[pk(7)], writes=[("YTM", t)])
                pending_tro.append(fin)
                adv(10000, force=True)

            pending_tro = []
            for _ in front(0):
                pass
            for t in range(NT):
                attention(t, front(t + 1) if t + 1 < NT else iter(()))
            while pending_tro:
                pending_tro.pop(0)()
            RSTD_LN[0] = False
            S.barrier()
            AR.reset()
            YTM2 = AR.alloc([4, L], BF16)
            WinR = AR.alloc([8, 2048], BF16)
            Wout = AR.alloc([8, 1024], BF16)
            junk = None
            hn = AR.alloc([1024], BF16)
            hTb = [AR.alloc([8, 128], BF16) for _ in range(2)]
            qa = AR.alloc([512], F32)
            qb = AR.alloc([512], F32)
            ka = AR.alloc([512], F32)
            kb = AR.alloc([512], F32)
            qtil = AR.alloc([4, 128], BF16)
            krr = AR.alloc([4, 128], BF16)
            khat = AR.alloc([4, 128], BF16)
            vbf = AR.alloc([4, 128], BF16)
            gs = AR.alloc([512], F32)
            qkT = AR.alloc([8, 128], BF16)
            PTr = AR.alloc([4, 128], BF16)
            S32 = AR.alloc([4, 128], F32)
            Sbf = AR.alloc([4, 128], BF16)
            osq = AR.alloc([512], F32)
            onr = AR.alloc([512], F32)
            yb = AR.alloc([512], BF16)
            yT = AR.alloc([4, 128], BF16)
            st8 = AR.alloc([32], F32)

            for kc in range(8):
                load_cast(WinR[:, kc, :], w_in_d[l, kc * 128:(kc + 1) * 128, 576:2624], 2048,
                          scale=gA[:, kc:kc + 1], dst_key="WinR", reads=["gcols"])
            for kc in range(8):
                load_cast(Wout[:, kc, :], wout_d[l, kc * 128:(kc + 1) * 128, :], 1024,
                          dst_key="Wout")
            S.op("pool", lambda e: e.memset(S32[:, :, :], 0.0), writes=["S32"])
            S.op("pool", lambda e: e.memset(Sbf[:, :, :], 0.0), writes=["Sbf"])

            pending_xadd = []
            for t in range(NT):
                n = ntok(t)
                c0 = tok0(t)
                sel = 1 if t == 0 else 0
                hT = hTb[t % 2]
                if t == 0:
                    norm_T(0, junk, hn, hT[:, :, 0:n], ("hT", 0), 4)
                pZ = bank(0, 4)

                def emit_z(tt):
                    nn = ntok(tt)
                    hTt = hTb[tt % 2]

                    def mmz(e):
                        for cc in range(4):
                            for kc in range(8):
                                e.matmul(pZ[:nn, cc * 512:(cc + 1) * 512], lhsT=hTt[:, kc, 0:nn],
                                         rhs=WinR[:, kc, cc * 512:(cc + 1) * 512],
                                         start=(kc == 0), stop=(kc == 7))
                    S.op("pe", mmz, reads=[("hT", tt % 2), "WinR"], writes=[pk(0), pk(1), pk(2), pk(3)])
                if t == 0:
                    emit_z(0)
                cosb = bc(cosR[:n, t, :].unsqueeze(1).unsqueeze(1), [n, 4, 2, 64])
                sinb = bc(sinR[:n, t, :].unsqueeze(1), [n, 4, 64])

                def rope(src_bank_key, src, ta, tb, nm):
                    x4 = src.rearrange("p (h two f) -> p h two f", h=4, two=2)
                    a4 = ta[:n, :].rearrange("p (h two f) -> p h two f", h=4, two=2)
                    b4 = tb[:n, :].rearrange("p (h two f) -> p h two f", h=4, two=2)
                    S.op("dve", lambda e: e.tensor_tensor(out=a4, in0=x4, in1=cosb, op=ALU.mult),
                         reads=[src_bank_key, "const"], writes=[nm + "a"])
                    S.op("dve", lambda e: e.tensor_tensor(out=b4[:, :, 0, :], in0=x4[:, :, 1, :], in1=sinb,
                                                          op=ALU.mult),
                         reads=[src_bank_key, "const"], writes=[nm + "b0"])
                    S.op("dve", lambda e: e.tensor_tensor(out=b4[:, :, 1, :], in0=x4[:, :, 0, :], in1=sinb,
                                                          op=ALU.mult),
                         reads=[src_bank_key, "const"], writes=[nm + "b1"])
                    S.op("dve", lambda e: e.tensor_tensor(out=a4[:, :, 0, :], in0=a4[:, :, 0, :],
                                                           in1=b4[:, :, 0, :], op=ALU.subtract),
                         reads=[nm + "a", nm + "b0"], writes=[nm + "a"])
                    S.op("dve", lambda e: e.tensor_tensor(out=a4[:, :, 1, :], in0=a4[:, :, 1, :],
                                                           in1=b4[:, :, 1, :], op=ALU.add),
                         reads=[nm + "a", nm + "b1"], writes=[nm + "a"])
                rope(pk(0), pZ[:n, 0:512], qa, qb, "rq")
                rope(pk(1), pZ[:n, 512:1024], ka, kb, "rk")
                qa3 = qa[:n, :].rearrange("p (h d) -> p h d", h=4)
                ka3 = ka[:n, :].rearrange("p (h d) -> p h d", h=4)
                S.op("dve", lambda e: e.tensor_tensor(out=qtil[:n, :, :], in0=qa3,
                                                       in1=bc(xi[:n, sel, :].unsqueeze(2), [n, 4, 128]),
                                                       op=ALU.mult),
                     reads=["rqa", "const"], writes=["qtil"])
                S.op("dve", lambda e: e.tensor_scalar(out=krr[:n, :, :], in0=ka3, scalar1=128.0 ** -0.5,
                                                       scalar2=None, op0=ALU.mult),
                     reads=["rka"], writes=["krr"])
                S.op("dve", lambda e: e.tensor_tensor(out=khat[:n, :, :], in0=ka3,
                                                       in1=bc(zeta[:n, sel, :].unsqueeze(2), [n, 4, 128]),
                                                       op=ALU.mult),
                     reads=["rka", "const"], writes=["khat"])
                while pending_xadd:
                    pending_xadd.pop(0)()
                S.op("act", lambda e: e.activation(out=vbf[:n, :, :].rearrange("p h d -> p (h d)"),
                                                   in_=pZ[:n, 1024:1536], func=AF.Copy),
                     reads=[pk(2)], writes=["vbf"])
                S.op("act", lambda e: e.activation(out=gs[:n, :], in_=pZ[:n, 1536:2048], func=AF.Silu),
                     reads=[pk(3)], writes=["gs"])
                if t + 1 < NT:
                    n1 = ntok(t + 1)
                    norm_T(t + 1, junk, hn, hTb[(t + 1) % 2][:, :, 0:n1], ("hT", (t + 1) % 2), 4)
                pT = bank16(4)

                def trqk(e):
                    ins = None
                    for h in range(4):
                        ins = e.transpose(out=pT[:, h * 128:h * 128 + n], in_=qtil[:n, h, :],
                                          identity=ident[:n, :n])
                    for h in range(4):
                        ins = e.transpose(out=pT[:, (4 + h) * 128:(4 + h) * 128 + n], in_=krr[:n, h, :],
                                          identity=ident[:n, :n])
                    return ins
                S.op("pe", trqk, reads=["qtil", "krr", "const"], writes=[pk(4)])
                S.op("dve", lambda e: e.tensor_copy(
                    out=qkT[:, :, 0:n], in_=pT.rearrange("p (k c) -> p k c", k=8)[:, :, 0:n]),
                    reads=[pk(4)], writes=["qkT"])
                pRS = bank(5).rearrange("p (h c) -> p h c", h=4)
                pSU = bank(6).rearrange("p (h c) -> p h c", h=4)
                pRO = bank(7).rearrange("p (h c) -> p h c", h=4)

                def mmrs(e):
                    ins = None
                    for h in range(4):
                        ins = e.matmul(pRS[:n, h, 0:n], lhsT=qkT[:, 4 + h, 0:n], rhs=qkT[:, h, 0:n],
                                       start=True, stop=True)
                    return ins
                S.op("pe", mmrs, reads=["qkT"], writes=[pk(5)])
                S.op("dve", lambda e: e.tensor_tensor(out=PTr[:n, :, 0:n], in0=pRS[:n, :, 0:n],
                                                      in1=dmask[:n, sel, :, 0:n], op=ALU.mult),
                     reads=[pk(5), "const"], writes=["PTr"])

                def mmro(e):
                    ins = None
                    for h in range(4):
                        e.matmul(pRO[:n, h, :], lhsT=PTr[:n, h, 0:n], rhs=vbf[:n, h, :], start=True, stop=False)
                        ins = e.matmul(pRO[:n, h, :], lhsT=qkT[:, h, 0:n], rhs=Sbf[:, h, :],
                                       start=False, stop=True)
                    return ins
                S.op("pe", mmro, reads=["PTr", "vbf", "qkT", "Sbf"], writes=[pk(7)])

                def mmsu(e):
                    ins = None
                    for h in range(4):
                        ins = e.matmul(pSU[:, h, :], lhsT=khat[:n, h, :], rhs=vbf[:n, h, :],
                                       start=True, stop=True)
                    return ins
                S.op("pe", mmsu, reads=["khat", "vbf"], writes=[pk(6)])
                if t + 1 < NT:
                    emit_z(t + 1)
                for h in range(4):
                    S.op("dve", lambda e, h=h: e.scalar_tensor_tensor(
                        out=S32[:, h, :], in0=S32[:, h, :], scalar=CD[h], in1=pSU[:, h, :],
                        op0=ALU.mult, op1=ALU.add),
                        reads=[pk(6), "S32"], writes=["S32"])
                S.op("pool", lambda e: e.tensor_copy(out=Sbf[:, :, :], in_=S32[:, :, :]),
                     reads=["S32"], writes=["Sbf"])
                S.op("dve", lambda e: e.tensor_reduce(out=st8[:n, 0:4], in_=pRO[:n, :, :], axis=AX.X,
                                                      op=ALU.add),
                     reads=[pk(7)], writes=["gsum"])
                S.op("act", lambda e: e.activation(out=osq[:n, :].rearrange("p (h d) -> p h d", h=4),
                                                   in_=pRO[:n, :, :], func=AF.Square),
                     reads=[pk(7)], writes=["osq"])
                S.op("dve", lambda e: e.tensor_reduce(
                    out=st8[:n, 4:8], in_=osq[:n, :].rearrange("p (h d) -> p h d", h=4), axis=AX.X,
                    op=ALU.add),
                    reads=["osq"], writes=["gssq"])
                S.op("dve", lambda e: e.tensor_scalar(out=st8[:n, 0:4], in0=st8[:n, 0:4], scalar1=1.0 / 128,
                                                      scalar2=None, op0=ALU.mult),
                     reads=["gsum"], writes=["gsum"])
                S.op("dve", lambda e: e.tensor_tensor(out=st8[:n, 8:12], in0=st8[:n, 0:4], in1=st8[:n, 0:4],
                                                      op=ALU.mult),
                     reads=["gsum"], writes=["gmsq"])
                S.op("dve", lambda e: e.scalar_tensor_tensor(out=st8[:n, 4:8], in0=st8[:n, 4:8],
                                                             scalar=1.0 / 128, in1=st8[:n, 8:12],
                                                             op0=ALU.mult, op1=ALU.subtract),
                     reads=["gssq", "gmsq"], writes=["gssq"])
                rstd_op(st8[:n, 4:8], st8[:n, 4:8], 1.0, ["gssq"], "gssq")
                S.op("dve", lambda e: e.scalar_tensor_tensor(out=st8[:n, 12:16], in0=st8[:n, 0:4],
                                                             scalar=-1.0, in1=st8[:n, 4:8],
                                                             op0=ALU.mult, op1=ALU.mult),
                     reads=["gsum", "gssq"], writes=["gnb"])
                for h in range(4):
                    S.op("dve", lambda e, h=h: e.tensor_scalar(
                        out=onr[:n, h * 128:(h + 1) * 128], in0=pRO[:n, h, :], scalar1=st8[:n, 4 + h:5 + h],
                        scalar2=st8[:n, 12 + h:13 + h], op0=ALU.mult, op1=ALU.add),
                        reads=[pk(7), "gssq", "gnb"], writes=["onr"])
                S.op("dve", lambda e: e.tensor_tensor(out=onr[:n, :], in0=onr[:n, :], in1=RG[:n, :],
                                                       op=ALU.mult),
                     reads=["onr", "gains"], writes=["onr"])
                S.op("dve", lambda e: e.tensor_tensor(out=onr[:n, :], in0=onr[:n, :], in1=RB[:n, :],
                                                       op=ALU.add),
                     reads=["onr", "gains"], writes=["onr"])
                S.op("dve", lambda e: e.tensor_tensor(out=yb[:n, :], in0=onr[:n, :], in1=gs[:n, :],
                                                       op=ALU.mult),
                     reads=["onr", "gs"], writes=["yb"])
                pT = bank16(4)

                def try_(e):
                    ins = None
                    for h in range(4):
                        ins = e.transpose(out=pT[:, h * 128:h * 128 + n], in_=yb[:n, h * 128:(h + 1) * 128],
                                          identity=ident[:n, :n])
                    return ins
                S.op("pe", try_, reads=["yb", "const"], writes=[pk(4)])
                S.op("act", lambda e: e.activation(
                    out=yT[:, :, 0:n], in_=pT[:, 0:512].rearrange("p (k c) -> p k c", k=4)[:, :, 0:n],
                    func=AF.Copy),
                    reads=[pk(4)], writes=["yT"])
                pW = bank(5, 2)

                def mmw(e):
                    ins = None
                    for hf in range(2):
                        for kc in range(8):
                            lt = YTM2[:, kc, c0:c0 + n] if kc < 4 else yT[:, kc - 4, 0:n]
                            ins = e.matmul(pW[:n, hf * 512:(hf + 1) * 512], lhsT=lt,
                                           rhs=Wout[:, kc, hf * 512:(hf + 1) * 512],
                                           start=(kc == 0), stop=(kc == 7))
                    return ins
                S.op("pe", mmw, reads=[("YTM", t), "yT", "Wout"], writes=[pk(5), pk(6)])
                def xadd(t=t, n=n, pW=pW):
                    S.op("dve", lambda e: e.tensor_tensor(out=X[:n, t, :], in0=X[:n, t, :], in1=pW[:n, :],
                                                          op=ALU.add),
                         reads=[("X", t), pk(5), pk(6)], writes=[("X", t)])
                pending_xadd.append(xadd)
            while pending_xadd:
                pending_xadd.pop(0)()
            if debug == "R":
                break
            S.barrier()
            AR.reset()
            hfT = AR.alloc([8, L], BF16)
            NSLOT = 8
            WGU = [AR.alloc([8, 256], BF16) for _ in range(NSLOT)]
            WD = [AR.alloc([1024], BF16) for _ in range(NSLOT)]
            junk = None
            hn = AR.alloc([1024], BF16)
            sg = [AR.alloc([512], F32) for _ in range(2)]
            actT = [AR.alloc([4, 512], BF16) for _ in range(2)]
            GFb = AR.alloc([1024], F32)
            S.dma("sp", "gains", [(GFb, ffng_d[l:l + 1, :].to_broadcast([128, D]))], writes=["GFb"])
            chunks = [[0]] + [list(range(1 + 4 * c, 5 + 4 * c)) for c in range(4)]

            def norm_chunk(cidx):
                if cidx >= len(chunks):
                    return
                for t_ in chunks[cidx]:
                    n_ = ntok(t_)
                    c0_ = tok0(t_)
                    norm_T(t_, junk, hn, hfT[:, :, c0_:c0_ + n_], ("hfT", t_), 4 + (t_ % 2), gain=GFb,
                           gain_key="GFb")
            norm_chunk(0)
            groups = [list(range(g, min(g + 4, NFC))) for g in range(0, NFC, 4)]
            ci = 0
            di = 0
            for g, grp in enumerate(groups):
                for j in grp:
                    sl = j % NSLOT
                    wsrc = wgu_d[l].rearrange("(k p) (two f) -> p k two f", p=128, two=2)
                    for two in range(2):
                        i = stage_i[0] % NSTAGE
                        stage_i[0] += 1
                        st = st_t[i]
                        stv = st[:, 0:1024].rearrange("p (k f) -> p k f", k=8)
                        S.dma("sp", "stage%d" % i, [(stv, wsrc[:, :, two, j * 128:(j + 1) * 128])],
                              writes=[("stage", i)])
                        S.op("act", lambda e, sl=sl, stv=stv, two=two: e.activation(
                            out=WGU[sl][:, :, two * 128:(two + 1) * 128], in_=stv, func=AF.Copy),
                            reads=[("stage", i)], writes=[("WGU", sl)])
                    load_cast(WD[sl][:, :], wd_d[l, j * 128:(j + 1) * 128, :], 1024, dst_key=("WD", sl))
                for chi, ch in enumerate(chunks):
                    cs = tok0(ch[0])
                    ncol = sum(ntok(t) for t in ch)
                    ab = actT[ci % 2]
                    abk = ("actT", ci % 2)
                    ci += 1
                    for jl, j in enumerate(grp):
                        sl = j % NSLOT
                        pb = 2 * (di % 2)
                        sgi = sg[di % 2]
                        sgk = ("sg", di % 2)
                        di += 1
                        pG = bank(pb)
                        pU = bank(pb + 1)
                        genN = None
                        if g == 0 and chi + 1 < len(chunks) and jl < len(chunks[chi + 1]):
                            t_ = chunks[chi + 1][jl]
                            n_ = ntok(t_)
                            c0_ = tok0(t_)
                            genN = norm_T_ops(t_, hn, hfT[:, :, c0_:c0_ + n_], ("hfT", t_), 4 + (t_ % 2),
                                              gain=GFb, gain_key="GFb")
                            for _ in range(3):
                                next(genN)

                        def mmgu(e, sl=sl, pG=pG, pU=pU):
                            ins = None
                            for kc in range(8):
                                ins = e.matmul(pG[:, 0:ncol], lhsT=WGU[sl][:, kc, 0:128],
                                               rhs=hfT[:, kc, cs:cs + ncol], start=(kc == 0), stop=(kc == 7))
                            for kc in range(8):
                                ins = e.matmul(pU[:, 0:ncol], lhsT=WGU[sl][:, kc, 128:256],
                                               rhs=hfT[:, kc, cs:cs + ncol], start=(kc == 0), stop=(kc == 7))
                            return ins
                        S.op("pe", mmgu, reads=[("WGU", sl)] + [("hfT", t) for t in ch],
                             writes=[pk(pb), pk(pb + 1)])
                        S.op("act", lambda e, pG=pG, sgi=sgi: e.activation(out=sgi[:, 0:ncol], in_=pG[:, 0:ncol],
                                                                          func=AF.Silu),
                             reads=[pk(pb)], writes=[sgk])
                        S.op("dve", lambda e, pU=pU, sgi=sgi, ab=ab, jl=jl: e.tensor_tensor(
                            out=ab[:, jl, 0:ncol], in0=sgi[:, 0:ncol], in1=pU[:, 0:ncol], op=ALU.mult),
                            reads=[sgk, pk(pb + 1)], writes=[abk])
                        if genN is not None:
                            for _ in genN:
                                pass
                    for t in ch:
                        n = ntok(t)
                        o0 = tok0(t) - cs
                        for hf in range(2):
                            pb = 4 + (hf + 2 * (t % 2))
                            pD = bank(pb)

                            def mmd(e, pD=pD, o0=o0, n=n, hf=hf, ab=ab):
                                ins = None
                                for jl, j in enumerate(grp):
                                    ins = e.matmul(pD[:n, :], lhsT=ab[:, jl, o0:o0 + n],
                                                   rhs=WD[j % NSLOT][:, hf * 512:(hf + 1) * 512],
                                                   start=(jl == 0), stop=(jl == len(grp) - 1))
                                return ins
                            S.op("pe", mmd, reads=[abk] + [("WD", j % NSLOT) for j in grp], writes=[pk(pb)])
                            S.op("dve", lambda e, pD=pD, n=n, t=t, hf=hf: e.tensor_tensor(
                                out=X[:n, t, hf * 512:(hf + 1) * 512], in0=X[:n, t, hf * 512:(hf + 1) * 512],
                                in1=pD[:n, :], op=ALU.add),
                                reads=[("X", t), pk(pb)], writes=[("X", t)])
                            if l == depth - 1 and g == len(groups) - 1 and hf == 1 and t >= 1:
                                S.dma("sp", "out", [(out_d[128 * (t - 1):128 * t, :], X[:, t, :])],
                                      reads=[("X", t)])
                                stored.add(t)
            S.barrier()

        if debug:
            S.final_wait("sp", "dbg")
        for t in range(1, NT):
            if t not in stored:
                S.dma("sp", "out", [(out_d[128 * (t - 1):128 * t, :], X[:, t, :])], reads=[("X", t)])
        S.final_wait("sp", "out")
        S.emit()
    return nc


def host_consts():
    pos = np.zeros((128, NT), dtype=np.float32)
    local = np.zeros((128, 2), dtype=np.float64)
    for t in range(NT):
        for p in range(128):
            pos[p, t] = tok0(t) + p if p < ntok(t) else 0
    p_idx = np.arange(128)
    local[:, 0] = p_idx
    local[:, 1] = np.minimum(112 + p_idx, 127)

    def tables(dim):
        inv = (10000.0 ** (-(np.arange(0, dim, 2, dtype=np.float32)) / dim)).astype(np.float32)
        ang = (pos[:, :, None] * inv[None, None, :]).astype(np.float32)
        return (np.cos(ang.astype(np.float64)).astype(np.float32).reshape(128, -1),
                np.sin(ang.astype(np.float64)).astype(np.float32).reshape(128, -1))
    cosR, sinR = tables(128)
    cosM, sinM = tables(64)
    lg = np.array([np.log(np.float32(g)) for g in GAMMA], dtype=np.float64)
    xi = np.zeros((128, 2, 4)); zeta = np.zeros((128, 2, 4)); dm = np.zeros((128, 2, 4, 128))
    nn = np.arange(128)
    for s in range(2):
        for h in range(4):
            xi[:, s, h] = np.exp((local[:, s] + 1.0) * lg[h])
            zeta[:, s, h] = np.exp((127.0 - local[:, s]) * lg[h]) * (128.0 ** -0.5)
            dm[:, s, h, :] = np.exp(-(local[:, s] + 1.0) * lg[h])[:, None] * (nn[None, :] >= nn[:, None])
    cm = (nn[None, :] >= nn[:, None]).astype(np.float32)
    return {
        "c_ident": np.eye(128, dtype=np.float32).astype(ml_dtypes.bfloat16),
        "c_cmask": cm.astype(ml_dtypes.bfloat16),
        "c_cosR": cosR, "c_sinR": sinR, "c_cosM": cosM, "c_sinM": sinM,
        "c_xi": xi.reshape(128, 8).astype(np.float32),
        "c_zeta": zeta.reshape(128, 8).astype(np.float32),
        "c_dmask": dm.reshape(128, -1).astype(np.float32),
    }


_NC_CACHE = {}


def kernel(**inputs):
    ins = {k: np.ascontiguousarray(np.asarray(v)) for k, v in inputs.items()}
    if "nc" not in _NC_CACHE:
        _NC_CACHE["nc"] = build()
    nc = _NC_CACHE["nc"]
    consts = host_consts()
    in_maps = []
    for b in range(8):
        m = {k: v for k, v in ins.items() if k != "x"}
        m["x"] = np.ascontiguousarray(ins["x"][b])
        m.update(consts)
        in_maps.append(m)
    res = run_bass_kernel_spmd(nc, in_maps, core_ids=list(range(8)))
    out = np.stack([np.asarray(r["out"]) for r in res.results], axis=0)
    return out.astype(np.float32)
```

```python
import contextlib
import numpy as np
import ml_dtypes
import concourse.bass as bass
import concourse.mybir as mybir
from concourse.bass_utils import run_bass_kernel_spmd

F32 = mybir.dt.float32
BF16 = mybir.dt.bfloat16
U8 = mybir.dt.uint8
AF = mybir.ActivationFunctionType
ALU = mybir.AluOpType
AX = mybir.AxisListType

D = 1024
SEQ = 2048
NMETA = 16
NT = 17
L = NMETA + SEQ
DEPTH = 2
NIN = 2624
DFF = 2816
NFC = DFF // 128
EPS = 1e-6
QKH = 192
GAMMA = [1.0 - 2.0 ** (-5.0 - h) for h in range(4)]
LOGG = [float(np.log(np.float32(g))) for g in GAMMA]
CD = [float(np.exp(128.0 * lg)) for lg in LOGG]
ATT_SCALE = QKH ** -0.5
SCORE_REORDER = False


def tok0(t):
    return 0 if t == 0 else NMETA + 128 * (t - 1)


def ntok(t):
    return NMETA if t == 0 else 128


class Sched:
    COMPUTE = ("pe", "act", "dve", "pool")

    def __init__(self, nc, es):
        self.nc = nc
        self.es = es
        self.eng = {"pe": nc.tensor, "act": nc.scalar, "dve": nc.vector,
                    "pool": nc.gpsimd, "sp": nc.sync}
        self.prog = {e: [] for e in self.eng}
        self.sems = {}
        self.total = {}
        self.waited = {e: {} for e in self.eng}
        self.lastw = {}
        self.readers = {}
        for e in self.COMPUTE:
            self.newsem(e)

    def newsem(self, name):
        self.sems[name] = self.es.enter_context(self.nc.semaphore("s_" + name))
        self.total[name] = 0
        return name

    def _deps(self, eng, reads, writes):
        deps = {}

        def add(s, v, raw):
            if s == eng:
                if eng == "pe" or not raw:
                    return
            if v > deps.get(s, 0):
                deps[s] = v

        for r in reads:
            ev = self.lastw.get(r)
            if ev is not None:
                add(ev[0], ev[1], True)
        for w in writes:
            ev = self.lastw.get(w)
            if ev is not None:
                add(ev[0], ev[1], False)
            for s, v in self.readers.get(w, {}).items():
                add(s, v, False)
        out = []
        for s, v in deps.items():
            if self.waited[eng].get(s, 0) < v:
                self.waited[eng][s] = v
                out.append((s, v))
        return out

    def _commit(self, ev, reads, writes):
        for w in writes:
            self.lastw[w] = ev
            self.readers[w] = {}
        for r in reads:
            d = self.readers.setdefault(r, {})
            if d.get(ev[0], 0) < ev[1]:
                d[ev[0]] = ev[1]

    def op(self, eng, fn, reads=(), writes=()):
        waits = self._deps(eng, reads, writes)
        self.total[eng] += 1
        rec = _Rec()
        fn(rec)
        calls = rec.calls

        def replay(e, calls=calls):
            ins = None
            for name, a, k in calls:
                ins = getattr(e, name)(*a, **k)
            return ins
        self.prog[eng].append((waits, replay, eng, 1))
        self._commit((eng, self.total[eng]), reads, writes)

    def dma(self, q, sem, pairs, reads=(), writes=(), slow=False):
        waits = self._deps(q, reads, writes)
        for i, (o, i_) in enumerate(pairs):
            if slow:
                f = (lambda e, o=o, i_=i_: e.dma_start(out=o, in_=i_, allow_slow_non_contiguous=True))
            else:
                f = (lambda e, o=o, i_=i_: e.dma_start(out=o, in_=i_))
            self.prog[q].append((waits if i == 0 else [], f, sem, 16))
        self.total[sem] += 16 * len(pairs)
        self._commit((sem, self.total[sem]), reads, writes)

    def barrier(self):
        for e in self.eng:
            waits = []
            for s, tot in self.total.items():
                if s != e and tot > 0 and self.waited[e].get(s, 0) < tot:
                    self.waited[e][s] = tot
                    waits.append((s, tot))
            if waits:
                self.prog[e].append((waits, None, None, 0))

    def final_wait(self, q, sem):
        self.prog[q].append(([(sem, self.total[sem])], None, None, 0))

    def emit(self):
        nc = self.nc
        with nc.Block() as block:
            for name, deco in (("pe", block.tensor), ("act", block.scalar),
                               ("dve", block.vector), ("pool", block.gpsimd),
                               ("sp", block.sync)):
                prog = self.prog[name]

                def body(e, prog=prog):
                    for waits, fn, sem, inc in prog:
                        for s, v in waits:
                            e.wait_ge(self.sems[s], v)
                        if fn is not None:
                            ins = fn(e)
                            ins.then_inc(self.sems[sem], inc)
                deco(body)


class _Rec:
    def __init__(self):
        self.calls = []

    def __getattr__(self, name):
        def f(*a, **k):
            self.calls.append((name, a, k))
            return None
        return f


class Arena:
    def __init__(self, ap_u8):
        self.ap = ap_u8
        self.size = ap_u8.shape[1]
        self.off = 0

    def alloc(self, shape, dtype):
        esz = 4 if dtype == F32 else 2
        n = int(np.prod(shape))
        nbytes = n * esz
        self.off = (self.off + 63) // 64 * 64
        assert self.off + nbytes <= self.size, (self.off, nbytes, self.size)
        v = self.ap[:, self.off:self.off + nbytes].bitcast(dtype)
        self.off += nbytes
        if len(shape) == 2:
            v = v.rearrange("p (a b) -> p a b", a=shape[0])
        elif len(shape) == 3:
            v = v.rearrange("p (a b c) -> p a b c", a=shape[0], b=shape[1])
        elif len(shape) == 4:
            v = v.rearrange("p (a b c d) -> p a b c d", a=shape[0], b=shape[1], c=shape[2])
        return v

    def mark(self):
        return self.off

    def reset(self, m=0):
        self.off = m


def bc(ap, shape):
    return ap.to_broadcast(list(shape))


def build(depth=DEPTH, debug=None):
    nc = bass.Bass("TRN2", target_bir_lowering=False)
    dr = {}

    def din(name, shape, dt=F32):
        dr[name] = nc.dram_tensor(name, list(shape), dt, kind="ExternalInput").ap()
        return dr[name]

    x_d = din("x", [SEQ, D])
    meta_d = din("meta_tokens", [NMETA, D])
    attn_g_d = din("attn_norm_g", [DEPTH, D])
    w_in_d = din("w_in", [DEPTH, D, NIN])
    qag_d = din("q_a_norm_g", [DEPTH, 256])
    wqb_d = din("w_q_b", [DEPTH, 256, 768])
    kvag_d = din("kv_a_norm_g", [DEPTH, 256])
    wkvb_d = din("w_kv_b", [DEPTH, 256, 1024])
    qg_d = din("q_norm_g", [DEPTH, QKH])
    kg_d = din("k_norm_g", [DEPTH, QKH])
    og_d = din("mla_out_norm_g", [DEPTH, 512])
    rg_d = din("ret_norm_g", [DEPTH, 512])
    rb_d = din("ret_norm_b", [DEPTH, 512])
    wout_d = din("w_out", [DEPTH, D, D])
    ffng_d = din("ffn_norm_g", [DEPTH, D])
    wgu_d = din("w_gate_up", [DEPTH, D, 2 * DFF])
    wd_d = din("w_down", [DEPTH, DFF, D])
    c_ident_d = din("c_ident", [128, 128], BF16)
    c_cmask_d = din("c_cmask", [128, 128], BF16)
    c_cosR_d = din("c_cosR", [128, NT * 64])
    c_sinR_d = din("c_sinR", [128, NT * 64])
    c_cosM_d = din("c_cosM", [128, NT * 32])
    c_sinM_d = din("c_sinM", [128, NT * 32])
    c_xi_d = din("c_xi", [128, 8])
    c_zeta_d = din("c_zeta", [128, 8])
    c_dmask_d = din("c_dmask", [128, 2 * 4 * 128])
    out_d = nc.dram_tensor("out", [SEQ, D], F32, kind="ExternalOutput").ap()
    dbg_d = None
    if debug:
        dbg_d = nc.dram_tensor("dbg", [128, 4096], F32, kind="ExternalOutput").ap()

    es = contextlib.ExitStack()
    with es:
        S = Sched(nc, es)
        def toap(h):
            try:
                return h.ap()
            except Exception:
                return h[:]
        X = toap(es.enter_context(nc.sbuf_tensor("X", [128, NT, D], F32)))
        CONST_BYTES = 26 * 1024
        STAGE_BYTES = 4096
        NSTAGE = 3
        ARENA_BYTES = 229344 - 16512 - NT * D * 4 - CONST_BYTES - NSTAGE * STAGE_BYTES
        cu8 = toap(es.enter_context(nc.sbuf_tensor("CONST", [128, CONST_BYTES], U8)))
        st_t = []
        for i in range(NSTAGE):
            s_ = es.enter_context(nc.sbuf_tensor("STAGE%d" % i, [128, STAGE_BYTES // 4], F32))
            st_t.append(toap(s_))
        au8 = toap(es.enter_context(nc.sbuf_tensor("ARENA", [128, ARENA_BYTES], U8)))
        PS = toap(es.enter_context(nc.psum_tensor("PS", [128, 4096], F32)))

        def bank(b, nb=1):
            return PS[:, 512 * b:512 * (b + nb)]

        def bank16(b):
            return PS[:, 512 * b:512 * (b + 1)].bitcast(BF16)

        def pk(b):
            return ("ps", b)

        CA = Arena(cu8)
        ident = CA.alloc([128], BF16)
        cmask = CA.alloc([128], BF16)
        cosR = CA.alloc([NT, 64], F32)
        sinR = CA.alloc([NT, 64], F32)
        cosM = CA.alloc([NT, 32], F32)
        sinM = CA.alloc([NT, 32], F32)
        xi = CA.alloc([2, 4], F32)
        zeta = CA.alloc([2, 4], F32)
        dmask = CA.alloc([2, 4, 128], F32)
        QG = CA.alloc([QKH], F32)
        KG = CA.alloc([QKH], F32)
        OG = CA.alloc([512], F32)
        RG = CA.alloc([512], F32)
        RB = CA.alloc([512], F32)
        gA = CA.alloc([8], F32)
        gF = CA.alloc([8], F32)
        gQ = CA.alloc([2], F32)
        gKV = CA.alloc([2], F32)
        stat = CA.alloc([64], F32)
        mhalf = CA.alloc([8], F32)
        AR = Arena(au8)

        for nm in ("const", "gains", "stage0", "stage1", "stage2", "out", "dbg"):
            S.newsem(nm)
        xsem = [S.newsem("x%d" % t) for t in range(NT)]

        S.dma("sp", "const", [
            (ident, c_ident_d), (cmask, c_cmask_d),
            (cosR, c_cosR_d.rearrange("p (a b) -> p a b", a=NT)),
            (sinR, c_sinR_d.rearrange("p (a b) -> p a b", a=NT)),
            (cosM, c_cosM_d.rearrange("p (a b) -> p a b", a=NT)),
            (sinM, c_sinM_d.rearrange("p (a b) -> p a b", a=NT)),
            (xi, c_xi_d.rearrange("p (a b) -> p a b", a=2)),
            (zeta, c_zeta_d.rearrange("p (a b) -> p a b", a=2)),
            (dmask, c_dmask_d.rearrange("p (a b c) -> p a b c", a=2, b=4)),
        ], writes=["const"])
        for t in range(NT):
            if t == 0:
                S.dma("sp", xsem[t], [(X[0:NMETA, 0, :], meta_d)], writes=[("X", 0)])
            else:
                S.dma("sp", xsem[t], [(X[:, t, :], x_d[128 * (t - 1):128 * t, :])],
                      writes=[("X", t)])

        S.op("pool", lambda e: e.memset(mhalf[:, :], -0.5), writes=["mhalf"])
        stage_i = [0]

        def load_cast(dst, src, ncols, scale=None, dst_key=None, reads=()):
            for a in range(0, ncols, 1024):
                b = min(ncols, a + 1024)
                i = stage_i[0] % NSTAGE
                stage_i[0] += 1
                st = st_t[i][:, 0:b - a]
                S.dma("sp", "stage%d" % i, [(st, src[:, a:b])], writes=[("stage", i)])
                d_ = dst[:, a:b]
                if scale is None:
                    S.op("act", lambda e, d_=d_, st=st: e.activation(out=d_, in_=st, func=AF.Copy),
                         reads=[("stage", i)] + list(reads), writes=[dst_key])
                else:
                    S.op("act", lambda e, d_=d_, st=st: e.activation(out=d_, in_=st, func=AF.Copy,
                                                                    scale=scale),
                         reads=[("stage", i)] + list(reads), writes=[dst_key])

        def load_gains(l):
            def bcast(src_row, n):
                return src_row.to_broadcast([128, n])
            S.dma("sp", "gains", [
                (QG, bcast(qg_d[l:l + 1, :], QKH)), (KG, bcast(kg_d[l:l + 1, :], QKH)),
                (OG, bcast(og_d[l:l + 1, :], 512)), (RG, bcast(rg_d[l:l + 1, :], 512)),
                (RB, bcast(rb_d[l:l + 1, :], 512)),
            ], writes=["gains"])
            S.dma("sp", "gains", [
                (gA, attn_g_d[l, :].rearrange("(k p) -> p k", p=128)),
                (gF, ffng_d[l, :].rearrange("(k p) -> p k", p=128)),
                (gQ, qag_d[l, :].rearrange("(k p) -> p k", p=128)),
                (gKV, kvag_d[l, :].rearrange("(k p) -> p k", p=128)),
            ], writes=["gcols"], slow=True)

        RSTD_LN = [False]

        def rstd_op(dst, src, inv_n, rkeys, wkey):
            n_, k_ = dst.shape
            S.op("pool", lambda e: e.tensor_scalar(out=dst, in0=src, scalar1=inv_n, scalar2=EPS,
                                                   op0=ALU.mult, op1=ALU.add),
                 reads=list(rkeys), writes=[wkey])
            S.op("pool", lambda e: e.tensor_tensor(out=dst, in0=dst, in1=mhalf[:n_, 0:k_], op=ALU.pow),
                 reads=[wkey, "mhalf"], writes=[wkey])

        def norm_T_ops(t, hn, dstT, dst_key, pb, gain=None, gain_key=None):
            n = ntok(t)
            ss = stat[:n, 0:1]
            rs = stat[:n, 1:2]
            S.op("act", lambda e: e.activation(out=hn[:n, :], in_=X[:n, t, :], func=AF.Square,
                                               accum_out=ss),
                 reads=[("X", t)], writes=["hn", "ss"])
            yield
            rstd_op(rs, ss, 1.0 / D, ["ss"], "rs")
            yield
            if gain is None:
                S.op("act", lambda e: e.activation(out=hn[:n, :], in_=X[:n, t, :], func=AF.Copy, scale=rs),
                     reads=[("X", t), "rs"], writes=["hn"])
            else:
                S.op("dve", lambda e: e.scalar_tensor_tensor(out=hn[:n, :], in0=X[:n, t, :], scalar=rs,
                                                             in1=gain[:n, :], op0=ALU.mult, op1=ALU.mult),
                     reads=[("X", t), "rs", gain_key], writes=["hn"])
            yield
            pT = bank16(pb)

            def tr(e):
                for kc in range(8):
                    e.transpose(out=pT[:, kc * 128:kc * 128 + n], in_=hn[:n, kc * 128:(kc + 1) * 128],
                                identity=ident[:n, :n])
            yield "pe"
            S.op("pe", tr, reads=["hn", "const"], writes=[pk(pb)])
            yield
            S.op("dve", lambda e: e.tensor_copy(
                out=dstT, in_=pT.rearrange("p (k c) -> p k c", k=8)[:, :, 0:n]),
                reads=[pk(pb)], writes=[dst_key])
            yield

        def norm_T(t, junk, hn, dstT, dst_key, pb, gain=None, gain_key=None):
            for _ in norm_T_ops(t, hn, dstT, dst_key, pb, gain, gain_key):
                pass

        dbg_off = [0]

        def dump(ap32, key, ncols, npart=128):
            o = dbg_off[0]
            S.dma("sp", "dbg", [(dbg_d[0:npart, o:o + ncols], ap32)], reads=[key])
            dbg_off[0] += ncols

        stored = set()
        for l in range(depth):
            load_gains(l)
            AR.reset()
            YTM = AR.alloc([4, L], BF16)
            WinM = AR.alloc([8, 576], BF16)
            Wqb = AR.alloc([2, 768], BF16)
            Wkvb = AR.alloc([2, 1024], BF16)
            KT = AR.alloc([6, L], BF16)
            V = AR.alloc([NT, 4, 130], BF16)
            junk = None
            hn = AR.alloc([1024], BF16)
            hT = AR.alloc([8, 128], BF16)
            cn = AR.alloc([512], BF16)
            cT = AR.alloc([4, 128], BF16)
            kpe = AR.alloc([64], F32)
            sq32 = AR.alloc([768], F32)
            qtmp = AR.alloc([4, QKH], F32)
            r1 = AR.alloc([256], F32)
            r2 = AR.alloc([256], F32)
            kr1 = AR.alloc([64], F32)
            kr2 = AR.alloc([64], F32)
            kr3 = AR.alloc([64], F32)
            qn = AR.alloc([4, 128], BF16)
            qp = AR.alloc([4, 64], BF16)
            kn = AR.alloc([4, 128], BF16)
            kp = AR.alloc([4, 64], BF16)
            QTb = [AR.alloc([4, 128], BF16) for _ in range(2)]
            QTzb = [AR.alloc([2, 2, 128], BF16) for _ in range(2)]
            PT = [AR.alloc([4, 128], BF16) for _ in range(2)]
            on = AR.alloc([4, 128], BF16)
            ocp = AR.alloc([516], F32)
            st8 = AR.alloc([16], F32)
            st9 = AR.alloc([8], F32)

            for kc in range(8):
                load_cast(WinM[:, kc, :], w_in_d[l, kc * 128:(kc + 1) * 128, 0:576], 576,
                          scale=gA[:, kc:kc + 1], dst_key="WinM", reads=["gcols"])
            for c in range(2):
                load_cast(Wqb[:, c, :], wqb_d[l, c * 128:(c + 1) * 128, :], 768,
                          scale=gQ[:, c:c + 1], dst_key="Wqb", reads=["gcols"])
                load_cast(Wkvb[:, c, :], wkvb_d[l, c * 128:(c + 1) * 128, :], 1024,
                          scale=gKV[:, c:c + 1], dst_key="Wkvb", reads=["gcols"])
            S.op("pool", lambda e: e.memset(V[:, :, :, 128:130], 1.0), writes=["Vones"])
            for i_ in range(2):
                S.op("pool", lambda e, i_=i_: e.memset(QTzb[i_][:, :, :, :], 0.0), writes=[("QT", i_)])
            print("pass M arena bytes used", AR.mark(), "of", AR.size)

            def front(t):
                n = ntok(t)
                c0 = tok0(t)
                QT = QTb[t % 2]
                QTz = QTzb[t % 2]
                qtk = ("QT", t % 2)
                for _ in norm_T_ops(t, hn, hT[:, :, 0:n], "hT", 2):
                    yield
                pZ = bank(0, 2)

                def mmz(e):
                    for (a, b) in ((0, 512), (512, 576)):
                        for kc in range(8):
                            e.matmul(pZ[:n, a:b], lhsT=hT[:, kc, 0:n], rhs=WinM[:, kc, a:b],
                                     start=(kc == 0), stop=(kc == 7))
                yield "pe"
                S.op("pe", mmz, reads=["hT", "WinM"], writes=[pk(0), pk(1)])
                yield
                ss2 = st8[:n, 0:2]
                rs2 = st8[:n, 2:4]
                S.op("act", lambda e: e.activation(out=sq32[:n, 0:256], in_=pZ[:n, 0:256], func=AF.Square,
                                                   accum_out=st8[:n, 0:1]),
                     reads=[pk(0)], writes=["sq32", "ssq"])
                yield
                S.op("act", lambda e: e.activation(out=sq32[:n, 256:512], in_=pZ[:n, 256:512], func=AF.Square,
                                                   accum_out=st8[:n, 1:2]),
                     reads=[pk(0)], writes=["sq32", "ssk"])
                yield
                rstd_op(rs2, ss2, 1.0 / 256, ["ssq", "ssk"], "rs2")
                yield
                S.op("act", lambda e: e.activation(out=cn[:n, 0:256], in_=pZ[:n, 0:256], func=AF.Copy,
                                                   scale=st8[:n, 2:3]),
                     reads=[pk(0), "rs2"], writes=["cn"])
                yield
                S.op("act", lambda e: e.activation(out=cn[:n, 256:512], in_=pZ[:n, 256:512], func=AF.Copy,
                                                   scale=st8[:n, 3:4]),
                     reads=[pk(0), "rs2"], writes=["cn"])
                yield
                S.op("dve", lambda e: e.tensor_copy(out=kpe[:n, :], in_=pZ[:n, 512:576]),
                     reads=[pk(1)], writes=["kpe"])
                yield
                pT = bank16(2)

                def trc(e):
                    for c in range(4):
                        e.transpose(out=pT[:, c * 128:c * 128 + n], in_=cn[:n, c * 128:(c + 1) * 128],
                                    identity=ident[:n, :n])
                yield "pe"
                S.op("pe", trc, reads=["cn", "const"], writes=[pk(2)])
                yield
                S.op("dve", lambda e: e.tensor_copy(
                    out=cT[:, :, 0:n], in_=pT[:, 0:512].rearrange("p (k c) -> p k c", k=4)[:, :, 0:n]),
                    reads=[pk(2)], writes=["cT"])
                yield
                pQ = bank(0, 2)
                pKV = bank(0, 2)

                def mmq(e):
                    for (a, b) in ((0, 512), (512, 768)):
                        for c in range(2):
                            e.matmul(pQ[:n, a:b], lhsT=cT[:, c, 0:n], rhs=Wqb[:, c, a:b],
                                     start=(c == 0), stop=(c == 1))
                yield "pe"
                S.op("pe", mmq, reads=["cT", "Wqb"], writes=[pk(0), pk(1)])
                yield
                pQ3 = pQ[:, 0:768].rearrange("p (h d) -> p h d", h=4)
                S.op("act", lambda e: e.activation(out=sq32[:n, 0:768], in_=pQ[:n, 0:768], func=AF.Square),
                     reads=[pk(0), pk(1)], writes=["sq32"])
                yield
                S.op("dve", lambda e: e.tensor_reduce(
                    out=st8[:n, 4:8], in_=sq32[:n, 0:768].rearrange("p (h d) -> p h d", h=4),
                    axis=AX.X, op=ALU.add),
                    reads=["sq32"], writes=["ssq4"])
                yield
                rstd_op(st8[:n, 4:8], st8[:n, 4:8], 1.0 / QKH, ["ssq4"], "ssq4")
                yield
                for h in range(4):
                    S.op("dve", lambda e, h=h: e.scalar_tensor_tensor(
                        out=qtmp[:n, h, :], in0=pQ3[:n, h, :], scalar=st8[:n, 4 + h:5 + h], in1=QG[:n, :],
                        op0=ALU.mult, op1=ALU.mult),
                        reads=[pk(0), pk(1), "ssq4", "gains"], writes=["qtmp"])
                    yield

                def mmkv(e):
                    for (a, b) in ((0, 512), (512, 1024)):
                        for c in range(2):
                            e.matmul(pKV[:n, a:b], lhsT=cT[:, 2 + c, 0:n], rhs=Wkvb[:, c, a:b],
                                     start=(c == 0), stop=(c == 1))
                yield "pe"
                S.op("pe", mmkv, reads=["cT", "Wkvb"], writes=[pk(0), pk(1)])
                yield
                S.op("dve", lambda e: e.tensor_copy(out=qn[:n, :, :], in_=qtmp[:n, :, 0:128]),
                     reads=["qtmp"], writes=["qn"])
                yield
                qpe4 = qtmp[:n, :, 128:192].rearrange("p h (two f) -> p h two f", two=2)
                cosb = bc(cosM[:n, t, :].unsqueeze(1).unsqueeze(1), [n, 4, 2, 32])
                sinb = bc(sinM[:n, t, :].unsqueeze(1), [n, 4, 32])
                r1v = r1[:n, :].rearrange("p (h two f) -> p h two f", h=4, two=2)
                r2v = r2[:n, :].rearrange("p (h two f) -> p h two f", h=4, two=2)
                S.op("dve", lambda e: e.tensor_tensor(out=r1v, in0=qpe4, in1=cosb, op=ALU.mult),
                     reads=["qtmp", "const"], writes=["r1"])
                yield
                S.op("dve", lambda e: e.tensor_tensor(out=r2v[:, :, 0, :], in0=qpe4[:, :, 1, :], in1=sinb,
                                                      op=ALU.mult),
                     reads=["qtmp", "const"], writes=["r2a"])
                yield
                S.op("dve", lambda e: e.tensor_tensor(out=r2v[:, :, 1, :], in0=qpe4[:, :, 0, :], in1=sinb,
                                                      op=ALU.mult),
                     reads=["qtmp", "const"], writes=["r2b"])
                yield
                qpv = qp[:n, :, :].rearrange("p h (two f) -> p h two f", two=2)
                S.op("dve", lambda e: e.tensor_tensor(out=qpv[:, :, 0, :], in0=r1v[:, :, 0, :],
                                                      in1=r2v[:, :, 0, :], op=ALU.subtract),
                     reads=["r1", "r2a"], writes=["qp"])
                yield
                S.op("dve", lambda e: e.tensor_tensor(out=qpv[:, :, 1, :], in0=r1v[:, :, 1, :],
                                                      in1=r2v[:, :, 1, :], op=ALU.add),
                     reads=["r1", "r2b"], writes=["qp"])
                yield
                def trq(e):
                    for h in range(4):
                        e.transpose(out=pT[:, h * 128:h * 128 + n], in_=qn[:n, h, :], identity=ident[:n, :n])
                    qpf = qp[:n, :, :].rearrange("p h f -> p (h f)")
                    for c in range(2):
                        e.transpose(out=pT[:, (4 + c) * 128:(4 + c) * 128 + n],
                                    in_=qpf[:, c * 128:(c + 1) * 128], identity=ident[:n, :n])
                yield "pe"
                S.op("pe", trq, reads=["qn", "qp", "const"], writes=[pk(2)])
                yield
                S.op("dve", lambda e: e.tensor_copy(
                    out=QT[:, :, 0:n], in_=pT[:, 0:512].rearrange("p (k c) -> p k c", k=4)[:, :, 0:n]),
                    reads=[pk(2)], writes=[qtk])
                yield
                pTz = pT[:, 512:768].rearrange("p (k c) -> p k c", k=2)
                S.op("dve", lambda e: e.tensor_copy(out=QTz[0:64, :, 0, 0:n], in_=pTz[0:64, :, 0:n]),
                     reads=[pk(2)], writes=[qtk])
                yield
                S.op("dve", lambda e: e.tensor_copy(out=QTz[64:128, :, 1, 0:n], in_=pTz[64:128, :, 0:n]),
                     reads=[pk(2)], writes=[qtk])
                yield
                pKV3 = pKV.rearrange("p (h d) -> p h d", h=4)
                S.op("act", lambda e: e.activation(
                    out=sq32[:n, 0:512].rearrange("p (h d) -> p h d", h=4), in_=pKV3[:n, :, 0:128],
                    func=AF.Square),
                    reads=[pk(0), pk(1)], writes=["sq32"])
                yield
                S.op("dve", lambda e: e.tensor_reduce(
                    out=st8[:n, 8:12], in_=sq32[:n, 0:512].rearrange("p (h d) -> p h d", h=4),
                    axis=AX.X, op=ALU.add),
                    reads=["sq32"], writes=["ssk4"])
                yield
                S.op("act", lambda e: e.activation(out=kr1[:n, :], in_=kpe[:n, :], func=AF.Square,
                                                   accum_out=st8[:n, 12:13]),
                     reads=["kpe"], writes=["kr1", "sskpe"])
                yield
                S.op("dve", lambda e: e.tensor_scalar(out=st8[:n, 8:12], in0=st8[:n, 8:12],
                                                      scalar1=st8[:n, 12:13], scalar2=None, op0=ALU.add),
                     reads=["ssk4", "sskpe"], writes=["ssk4"])
                yield
                rstd_op(st8[:n, 8:12], st8[:n, 8:12], 1.0 / QKH, ["ssk4"], "ssk4")
                yield
                for h in range(4):
                    S.op("dve", lambda e, h=h: e.scalar_tensor_tensor(
                        out=kn[:n, h, :], in0=pKV3[:n, h, 0:128], scalar=st8[:n, 8 + h:9 + h],
                        in1=KG[:n, 0:128], op0=ALU.mult, op1=ALU.mult),
                        reads=[pk(0), pk(1), "ssk4", "gains"], writes=["kn"])
                    yield
                S.op("act", lambda e: e.activation(out=V[:n, t, :, 0:128], in_=pKV3[:n, :, 128:256],
                                                   func=AF.Copy),
                     reads=[pk(0), pk(1)], writes=[("V", t)])
                yield
                S.op("dve", lambda e: e.tensor_tensor(out=kr1[:n, :], in0=kpe[:n, :], in1=KG[:n, 128:192],
                                                      op=ALU.mult),
                     reads=["kpe", "gains", "sskpe"], writes=["kr1"])
                yield
                k1v = kr1[:n, :].rearrange("p (two f) -> p two f", two=2)
                k2v = kr2[:n, :].rearrange("p (two f) -> p two f", two=2)
                k3v = kr3[:n, :].rearrange("p (two f) -> p two f", two=2)
                cosb2 = bc(cosM[:n, t, :].unsqueeze(1), [n, 2, 32])
                S.op("dve", lambda e: e.tensor_tensor(out=k2v, in0=k1v, in1=cosb2, op=ALU.mult),
                     reads=["kr1", "const"], writes=["kr2"])
                yield
                S.op("dve", lambda e: e.tensor_tensor(out=k3v[:, 0, :], in0=k1v[:, 1, :], in1=sinM[:n, t, :],
                                                      op=ALU.mult),
                     reads=["kr1", "const"], writes=["kr3a"])
                yield
                S.op("dve", lambda e: e.tensor_tensor(out=k3v[:, 1, :], in0=k1v[:, 0, :], in1=sinM[:n, t, :],
                                                      op=ALU.mult),
                     reads=["kr1", "const"], writes=["kr3b"])
                yield
                S.op("dve", lambda e: e.tensor_tensor(out=k2v[:, 0, :], in0=k2v[:, 0, :], in1=k3v[:, 0, :],
                                                      op=ALU.subtract),
                     reads=["kr2", "kr3a"], writes=["kr2"])
                yield
                S.op("dve", lambda e: e.tensor_tensor(out=k2v[:, 1, :], in0=k2v[:, 1, :], in1=k3v[:, 1, :],
                                                      op=ALU.add),
                     reads=["kr2", "kr3b"], writes=["kr2"])
                yield
                S.op("dve", lambda e: e.tensor_tensor(
                    out=kp[:n, :, :], in0=bc(kr2[:n, :].unsqueeze(1), [n, 4, 64]),
                    in1=bc(st8[:n, 8:12].unsqueeze(2), [n, 4, 64]), op=ALU.mult),
                    reads=["kr2", "ssk4"], writes=["kp"])
                yield

                def trk(e):
                    for h in range(4):
                        e.transpose(out=pT[:, h * 128:h * 128 + n], in_=kn[:n, h, :], identity=ident[:n, :n])
                    kpf = kp[:n, :, :].rearrange("p h f -> p (h f)")
                    for c in range(2):
                        e.transpose(out=pT[:, (4 + c) * 128:(4 + c) * 128 + n],
                                    in_=kpf[:, c * 128:(c + 1) * 128], identity=ident[:n, :n])
                yield "pe"
                S.op("pe", trk, reads=["kn", "kp", "const"], writes=[pk(2)])
                yield
                S.op("act", lambda e: e.activation(
                    out=KT[:, :, c0:c0 + n], in_=pT[:, 0:768].rearrange("p (k c) -> p k c", k=6)[:, :, 0:n],
                    func=AF.Copy),
                    reads=[pk(2)], writes=[("KT", t)])
                yield

            def attention(t, gen):
                n = ntok(t)
                c0 = tok0(t)
                QT = QTb[t % 2]
                QTz = QTzb[t % 2]
                qtk = ("QT", t % 2)
                spj = -(-76 // (t + 1))

                budget = [0]

                def adv(k, force=False):
                    budget[0] += k
                    while budget[0] > 0:
                        try:
                            r = next(gen)
                        except StopIteration:
                            return
                        if r == "pe":
                            if not force:
                                return
                            continue
                        budget[0] -= 1

                def scores(j):
                    m = ntok(j)
                    k0 = tok0(j)
                    sb = 3 + (j % 2)
                    pS3 = bank(sb).rearrange("p (h c) -> p h c", h=4)

                    def mms(e):
                        for h in range(4):
                            e.matmul(pS3[:m, h, 0:n], lhsT=KT[:, h, k0:k0 + m], rhs=QT[:, h, 0:n],
                                     start=(h == 0), stop=False, skip_group_check=True)
                        for hp in range(2):
                            e.matmul(pS3[:m, 2 * hp:2 * hp + 2, 0:n], lhsT=KT[:, 4 + hp, k0:k0 + m],
                                     rhs=QTz[:, hp, :, 0:n], start=False, stop=True, skip_group_check=True)
                    S.op("pe", mms, reads=[("KT", j), qtk], writes=[pk(sb)])

                def obank(h):
                    return bank(5 + h // 2)[:, (h % 2) * 129:(h % 2) * 129 + 129]

                scores(0)
                for j in range(t + 1):
                    m = ntok(j)
                    sb = 3 + (j % 2)
                    pS3 = bank(sb).rearrange("p (h c) -> p h c", h=4)
                    pt = PT[j % 2]
                    ptk = ("PT", j % 2)
                    if j + 1 <= t:
                        scores(j + 1)
                    S.op("act", lambda e: e.activation(out=pt[:m, :, 0:n], in_=pS3[:m, :, 0:n],
                                                       func=AF.Exp, scale=ATT_SCALE),
                         reads=[pk(sb)], writes=[ptk])
                    if j == t:
                        S.op("dve", lambda e: e.tensor_tensor(
                            out=pt[:m, :, 0:n], in0=pt[:m, :, 0:n],
                            in1=bc(cmask[:m, 0:n].unsqueeze(1), [m, 4, n]), op=ALU.mult),
                            reads=[ptk, "const"], writes=[ptk])

                    def mmpv(e):
                        for h in range(4):
                            e.matmul(obank(h)[:n, :], lhsT=pt[:m, h, 0:n], rhs=V[:m, j, h, 0:129],
                                     start=(j == 0 and h % 2 == 0), stop=(j == t), skip_group_check=True)
                    S.op("pe", mmpv, reads=[ptk, ("V", j), "Vones"], writes=[pk(5), pk(6)])
                    adv(spj)
                    if j == 0:
                        while pending_tro:
                            pending_tro.pop(0)()
                ocp3 = ocp[:n, :].rearrange("p (h c) -> p h c", h=4)
                S.op("dve", lambda e: e.tensor_copy(out=ocp[:n, 0:258], in_=bank(5)[:n, 0:258]),
                     reads=[pk(5)], writes=["ocp"])
                S.op("dve", lambda e: e.tensor_copy(out=ocp[:n, 258:516], in_=bank(6)[:n, 0:258]),
                     reads=[pk(6)], writes=["ocp"])
                adv(2)
                S.op("dve", lambda e: e.reciprocal(out=st9[:n, 0:4], in_=ocp3[:, :, 128]),
                     reads=["ocp"], writes=["rden"])
                S.op("dve", lambda e: e.tensor_tensor(out=ocp3[:, :, 0:128], in0=ocp3[:, :, 0:128],
                                                      in1=bc(st9[:n, 0:4].unsqueeze(2), [n, 4, 128]), op=ALU.mult),
                     reads=["ocp", "rden"], writes=["ocp"])
                adv(2)
                for h in range(4):
                    S.op("act", lambda e: e.activation(out=on[:n, h, :], in_=ocp3[:, h, 0:128],
                                                       func=AF.Square, accum_out=st9[:n, 4 + h:5 + h]),
                         reads=["ocp"], writes=["on", "oss"])
                adv(2)
                rstd_op(st9[:n, 4:8], st9[:n, 4:8], 1.0 / 128, ["oss"], "oss")
                S.op("dve", lambda e: e.tensor_tensor(out=ocp3[:, :, 0:128], in0=ocp3[:, :, 0:128],
                                                      in1=bc(st9[:n, 4:8].unsqueeze(2), [n, 4, 128]), op=ALU.mult),
                     reads=["ocp", "oss"], writes=["ocp"])
                S.op("dve", lambda e: e.tensor_tensor(out=on[:n, :, :], in0=ocp3[:, :, 0:128],
                                                      in1=OG[:n, :].rearrange("p (h c) -> p h c", h=4),
                                                      op=ALU.mult),
                     reads=["ocp", "gains"], writes=["on"])
                adv(2)
                def fin(t=t, n=n, c0=c0):
                    pT7 = bank16(7)

                    def tro(e):
                        for h in range(4):
                            e.transpose(out=pT7[:, h * 128:h * 128 + n], in_=on[:n, h, :], identity=ident[:n, :n])
                    S.op("pe", tro, reads=["on", "const"], writes=[pk(7)])
                    S.op("act", lambda e: e.activation(
                        out=YTM[:, :, c0:c0 + n],
                        in_=pT7[:, 0:512].rearrange("p (k c) -> p k c", k=4)[:, :, 0:n], func=AF.Copy),
                        reads=[pk(7)], writes=[("YTM", t)])
                pending_tro.append(fin)
                adv(10000, force=True)

            pending_tro = []
            for _ in front(0):
                pass
            for t in range(NT):
                attention(t, front(t + 1) if t + 1 < NT else iter(()))
            while pending_tro:
                pending_tro.pop(0)()
            RSTD_LN[0] = False
            S.barrier()
            AR.reset()
            YTM2 = AR.alloc([4, L], BF16)
            WinR = AR.alloc([8, 2048], BF16)
            Wout = AR.alloc([8, 1024], BF16)
            junk = None
            hn = AR.alloc([1024], BF16)
            hTb = [AR.alloc([8, 128], BF16) for _ in range(2)]
            qa = AR.alloc([512], F32)
            qb = AR.alloc([512], F32)
            ka = AR.alloc([512], F32)
            kb = AR.alloc([512], F32)
            qtil = AR.alloc([4, 128], BF16)
            krr = AR.alloc([4, 128], BF16)
            khat = AR.alloc([4, 128], BF16)
            vbf = AR.alloc([4, 128], BF16)
            gs = AR.alloc([512], F32)
            qkT = AR.alloc([8, 128], BF16)
            PTr = AR.alloc([4, 128], BF16)
            S32 = AR.alloc([4, 128], F32)
            Sbf = AR.alloc([4, 128], BF16)
            osq = AR.alloc([512], F32)
            onr = AR.alloc([512], F32)
            yb = AR.alloc([512], BF16)
            yT = AR.alloc([4, 128], BF16)
            st8 = AR.alloc([32], F32)

            for kc in range(8):
                load_cast(WinR[:, kc, :], w_in_d[l, kc * 128:(kc + 1) * 128, 576:2624], 2048,
                          scale=gA[:, kc:kc + 1], dst_key="WinR", reads=["gcols"])
            for kc in range(8):
                load_cast(Wout[:, kc, :], wout_d[l, kc * 128:(kc + 1) * 128, :], 1024,
                          dst_key="Wout")
            S.op("pool", lambda e: e.memset(S32[:, :, :], 0.0), writes=["S32"])
            S.op("pool", lambda e: e.memset(Sbf[:, :, :], 0.0), writes=["Sbf"])

            pending_xadd = []
            for t in range(NT):
                n = ntok(t)
                c0 = tok0(t)
                sel = 1 if t == 0 else 0
                hT = hTb[t % 2]
                if t == 0:
                    norm_T(0, junk, hn, hT[:, :, 0:n], ("hT", 0), 4)
                pZ = bank(0, 4)

                def emit_z(tt):
                    nn = ntok(tt)
                    hTt = hTb[tt % 2]

                    def mmz(e):
                        for cc in range(4):
                            for kc in range(8):
                                e.matmul(pZ[:nn, cc * 512:(cc + 1) * 512], lhsT=hTt[:, kc, 0:nn],
                                         rhs=WinR[:, kc, cc * 512:(cc + 1) * 512],
                                         start=(kc == 0), stop=(kc == 7))
                    S.op("pe", mmz, reads=[("hT", tt % 2), "WinR"], writes=[pk(0), pk(1), pk(2), pk(3)])
                if t == 0:
                    emit_z(0)
                cosb = bc(cosR[:n, t, :].unsqueeze(1).unsqueeze(1), [n, 4, 2, 64])
                sinb = bc(sinR[:n, t, :].unsqueeze(1), [n, 4, 64])

                def rope(src_bank_key, src, ta, tb, nm):
                    x4 = src.rearrange("p (h two f) -> p h two f", h=4, two=2)
                    a4 = ta[:n, :].rearrange("p (h two f) -> p h two f", h=4, two=2)
                    b4 = tb[:n, :].rearrange("p (h two f) -> p h two f", h=4, two=2)
                    S.op("dve", lambda e: e.tensor_tensor(out=a4, in0=x4, in1=cosb, op=ALU.mult),
                         reads=[src_bank_key, "const"], writes=[nm + "a"])
                    S.op("dve", lambda e: e.tensor_tensor(out=b4[:, :, 0, :], in0=x4[:, :, 1, :], in1=sinb,
                                                          op=ALU.mult),
                         reads=[src_bank_key, "const"], writes=[nm + "b0"])
                    S.op("dve", lambda e: e.tensor_tensor(out=b4[:, :, 1, :], in0=x4[:, :, 0, :], in1=sinb,
                                                          op=ALU.mult),
                         reads=[src_bank_key, "const"], writes=[nm + "b1"])
                    S.op("dve", lambda e: e.tensor_tensor(out=a4[:, :, 0, :], in0=a4[:, :, 0, :],
                                                           in1=b4[:, :, 0, :], op=ALU.subtract),
                         reads=[nm + "a", nm + "b0"], writes=[nm + "a"])
                    S.op("dve", lambda e: e.tensor_tensor(out=a4[:, :, 1, :], in0=a4[:, :, 1, :],
                                                           in1=b4[:, :, 1, :], op=ALU.add),
                         reads=[nm + "a", nm + "b1"], writes=[nm + "a"])
                rope(pk(0), pZ[:n, 0:512], qa, qb, "rq")
                rope(pk(1), pZ[:n, 512:1024], ka, kb, "rk")
                qa3 = qa[:n, :].rearrange("p (h d) -> p h d", h=4)
                ka3 = ka[:n, :].rearrange("p (h d) -> p h d", h=4)
                S.op("dve", lambda e: e.tensor_tensor(out=qtil[:n, :, :], in0=qa3,
                                                       in1=bc(xi[:n, sel, :].unsqueeze(2), [n, 4, 128]),
                                                       op=ALU.mult),
                     reads=["rqa", "const"], writes=["qtil"])
                S.op("dve", lambda e: e.tensor_scalar(out=krr[:n, :, :], in0=ka3, scalar1=128.0 ** -0.5,
                                                       scalar2=None, op0=ALU.mult),
                     reads=["rka"], writes=["krr"])
                S.op("dve", lambda e: e.tensor_tensor(out=khat[:n, :, :], in0=ka3,
                                                       in1=bc(zeta[:n, sel, :].unsqueeze(2), [n, 4, 128]),
                                                       op=ALU.mult),
                     reads=["rka", "const"], writes=["khat"])
                while pending_xadd:
                    pending_xadd.pop(0)()
                S.op("act", lambda e: e.activation(out=vbf[:n, :, :].rearrange("p h d -> p (h d)"),
                                                   in_=pZ[:n, 1024:1536], func=AF.Copy),
                     reads=[pk(2)], writes=["vbf"])
                S.op("act", lambda e: e.activation(out=gs[:n, :], in_=pZ[:n, 1536:2048], func=AF.Silu),
                     reads=[pk(3)], writes=["gs"])
                if t + 1 < NT:
                    n1 = ntok(t + 1)
                    norm_T(t + 1, junk, hn, hTb[(t + 1) % 2][:, :, 0:n1], ("hT", (t + 1) % 2), 4)
                pT = bank16(4)

                def trqk(e):
                    ins = None
                    for h in range(4):
                        ins = e.transpose(out=pT[:, h * 128:h * 128 + n], in_=qtil[:n, h, :],
                                          identity=ident[:n, :n])
                    for h in range(4):
                        ins = e.transpose(out=pT[:, (4 + h) * 128:(4 + h) * 128 + n], in_=krr[:n, h, :],
                                          identity=ident[:n, :n])
                    return ins
                S.op("pe", trqk, reads=["qtil", "krr", "const"], writes=[pk(4)])
                S.op("dve", lambda e: e.tensor_copy(
                    out=qkT[:, :, 0:n], in_=pT.rearrange("p (k c) -> p k c", k=8)[:, :, 0:n]),
                    reads=[pk(4)], writes=["qkT"])
                pRS = bank(5).rearrange("p (h c) -> p h c", h=4)
                pSU = bank(6).rearrange("p (h c) -> p h c", h=4)
                pRO = bank(7).rearrange("p (h c) -> p h c", h=4)

                def mmrs(e):
                    ins = None
                    for h in range(4):
                        ins = e.matmul(pRS[:n, h, 0:n], lhsT=qkT[:, 4 + h, 0:n], rhs=qkT[:, h, 0:n],
                                       start=True, stop=True)
                    return ins
                S.op("pe", mmrs, reads=["qkT"], writes=[pk(5)])
                S.op("dve", lambda e: e.tensor_tensor(out=PTr[:n, :, 0:n], in0=pRS[:n, :, 0:n],
                                                      in1=dmask[:n, sel, :, 0:n], op=ALU.mult),
                     reads=[pk(5), "const"], writes=["PTr"])

                def mmro(e):
                    ins = None
                    for h in range(4):
                        e.matmul(pRO[:n, h, :], lhsT=PTr[:n, h, 0:n], rhs=vbf[:n, h, :], start=True, stop=False)
                        ins = e.matmul(pRO[:n, h, :], lhsT=qkT[:, h, 0:n], rhs=Sbf[:, h, :],
                                       start=False, stop=True)
                    return ins
                S.op("pe", mmro, reads=["PTr", "vbf", "qkT", "Sbf"], writes=[pk(7)])

                def mmsu(e):
                    ins = None
                    for h in range(4):
                        ins = e.matmul(pSU[:, h, :], lhsT=khat[:n, h, :], rhs=vbf[:n, h, :],
                                       start=True, stop=True)
                    return ins
                S.op("pe", mmsu, reads=["khat", "vbf"], writes=[pk(6)])
                if t + 1 < NT:
                    emit_z(t + 1)
                for h in range(4):
                    S.op("dve", lambda e, h=h: e.scalar_tensor_tensor(
                        out=S32[:, h, :], in0=S32[:, h, :], scalar=CD[h], in1=pSU[:, h, :],
                        op0=ALU.mult, op1=ALU.add),
                        reads=[pk(6), "S32"], writes=["S32"])
                S.op("pool", lambda e: e.tensor_copy(out=Sbf[:, :, :], in_=S32[:, :, :]),
                     reads=["S32"], writes=["Sbf"])
                S.op("dve", lambda e: e.tensor_reduce(out=st8[:n, 0:4], in_=pRO[:n, :, :], axis=AX.X,
                                                      op=ALU.add),
                     reads=[pk(7)], writes=["gsum"])
                S.op("act", lambda e: e.activation(out=osq[:n, :].rearrange("p (h d) -> p h d", h=4),
                                                   in_=pRO[:n, :, :], func=AF.Square),
                     reads=[pk(7)], writes=["osq"])
                S.op("dve", lambda e: e.tensor_reduce(
                    out=st8[:n, 4:8], in_=osq[:n, :].rearrange("p (h d) -> p h d", h=4), axis=AX.X,
                    op=ALU.add),
                    reads=["osq"], writes=["gssq"])
                S.op("dve", lambda e: e.tensor_scalar(out=st8[:n, 0:4], in0=st8[:n, 0:4], scalar1=1.0 / 128,
                                                      scalar2=None, op0=ALU.mult),
                     reads=["gsum"], writes=["gsum"])
                S.op("dve", lambda e: e.tensor_tensor(out=st8[:n, 8:12], in0=st8[:n, 0:4], in1=st8[:n, 0:4],
                                                      op=ALU.mult),
                     reads=["gsum"], writes=["gmsq"])
                S.op("dve", lambda e: e.scalar_tensor_tensor(out=st8[:n, 4:8], in0=st8[:n, 4:8],
                                                             scalar=1.0 / 128, in1=st8[:n, 8:12],
                                                             op0=ALU.mult, op1=ALU.subtract),
                     reads=["gssq", "gmsq"], writes=["gssq"])
                rstd_op(st8[:n, 4:8], st8[:n, 4:8], 1.0, ["gssq"], "gssq")
                S.op("dve", lambda e: e.scalar_tensor_tensor(out=st8[:n, 12:16], in0=st8[:n, 0:4],
                                                             scalar=-1.0, in1=st8[:n, 4:8],
                                                             op0=ALU.mult, op1=ALU.mult),
                     reads=["gsum", "gssq"], writes=["gnb"])
                for h in range(4):
                    S.op("dve", lambda e, h=h: e.tensor_scalar(
                        out=onr[:n, h * 128:(h + 1) * 128], in0=pRO[:n, h, :], scalar1=st8[:n, 4 + h:5 + h],
                        scalar2=st8[:n, 12 + h:13 + h], op0=ALU.mult, op1=ALU.add),
                        reads=[pk(7), "gssq", "gnb"], writes=["onr"])
                S.op("dve", lambda e: e.tensor_tensor(out=onr[:n, :], in0=onr[:n, :], in1=RG[:n, :],
                                                       op=ALU.mult),
                     reads=["onr", "gains"], writes=["onr"])
                S.op("dve", lambda e: e.tensor_tensor(out=onr[:n, :], in0=onr[:n, :], in1=RB[:n, :],
                                                       op=ALU.add),
                     reads=["onr", "gains"], writes=["onr"])
                S.op("dve", lambda e: e.tensor_tensor(out=yb[:n, :], in0=onr[:n, :], in1=gs[:n, :],
                                                       op=ALU.mult),
                     reads=["onr", "gs"], writes=["yb"])
                pT = bank16(4)

                def try_(e):
                    ins = None
                    for h in range(4):
                        ins = e.transpose(out=pT[:, h * 128:h * 128 + n], in_=yb[:n, h * 128:(h + 1) * 128],
                                          identity=ident[:n, :n])
                    return ins
                S.op("pe", try_, reads=["yb", "const"], writes=[pk(4)])
                S.op("act", lambda e: e.activation(
                    out=yT[:, :, 0:n], in_=pT[:, 0:512].rearrange("p (k c) -> p k c", k=4)[:, :, 0:n],
                    func=AF.Copy),
                    reads=[pk(4)], writes=["yT"])
                pW = bank(5, 2)

                def mmw(e):
                    ins = None
                    for hf in range(2):
                        for kc in range(8):
                            lt = YTM2[:, kc, c0:c0 + n] if kc < 4 else yT[:, kc - 4, 0:n]
                            ins = e.matmul(pW[:n, hf * 512:(hf + 1) * 512], lhsT=lt,
                                           rhs=Wout[:, kc, hf * 512:(hf + 1) * 512],
                                           start=(kc == 0), stop=(kc == 7))
                    return ins
                S.op("pe", mmw, reads=[("YTM", t), "yT", "Wout"], writes=[pk(5), pk(6)])
                def xadd(t=t, n=n, pW=pW):
                    S.op("dve", lambda e: e.tensor_tensor(out=X[:n, t, :], in0=X[:n, t, :], in1=pW[:n, :],
                                                          op=ALU.add),
                         reads=[("X", t), pk(5), pk(6)], writes=[("X", t)])
                pending_xadd.append(xadd)
            while pending_xadd:
                pending_xadd.pop(0)()
            if debug == "R":
                break
            S.barrier()
            AR.reset()
            hfT = AR.alloc([8, L], BF16)
            NSLOT = 8
            WGU = [AR.alloc([8, 256], BF16) for _ in range(NSLOT)]
            WD = [AR.alloc([1024], BF16) for _ in range(NSLOT)]
            junk = None
            hn = AR.alloc([1024], BF16)
            sg = [AR.alloc([512], F32) for _ in range(2)]
            actT = [AR.alloc([4, 512], BF16) for _ in range(2)]
            GFb = AR.alloc([1024], F32)
            S.dma("sp", "gains", [(GFb, ffng_d[l:l + 1, :].to_broadcast([128, D]))], writes=["GFb"])
            chunks = [[0]] + [list(range(1 + 4 * c, 5 + 4 * c)) for c in range(4)]
            if l == depth - 1:
                chunks = chunks[1:]

            def norm_chunk(cidx):
                if cidx >= len(chunks):
                    return
                for t_ in chunks[cidx]:
                    n_ = ntok(t_)
                    c0_ = tok0(t_)
                    norm_T(t_, junk, hn, hfT[:, :, c0_:c0_ + n_], ("hfT", t_), 4 + (t_ % 2), gain=GFb,
                           gain_key="GFb")
            norm_chunk(0)
            groups = [list(range(g, min(g + 4, NFC))) for g in range(0, NFC, 4)]
            ci = 0
            di = 0
            for g, grp in enumerate(groups):
                for j in grp:
                    sl = j % NSLOT
                    wsrc = wgu_d[l].rearrange("(k p) (two f) -> p k two f", p=128, two=2)
                    for two in range(2):
                        i = stage_i[0] % NSTAGE
                        stage_i[0] += 1
                        st = st_t[i]
                        stv = st[:, 0:1024].rearrange("p (k f) -> p k f", k=8)
                        S.dma("sp", "stage%d" % i, [(stv, wsrc[:, :, two, j * 128:(j + 1) * 128])],
                              writes=[("stage", i)])
                        S.op("act", lambda e, sl=sl, stv=stv, two=two: e.activation(
                            out=WGU[sl][:, :, two * 128:(two + 1) * 128], in_=stv, func=AF.Copy),
                            reads=[("stage", i)], writes=[("WGU", sl)])
                    load_cast(WD[sl][:, :], wd_d[l, j * 128:(j + 1) * 128, :], 1024, dst_key=("WD", sl))
                for chi, ch in enumerate(chunks):
                    cs = tok0(ch[0])
                    ncol = sum(ntok(t) for t in ch)
                    ab = actT[ci % 2]
                    abk = ("actT", ci % 2)
                    ci += 1
                    for jl, j in enumerate(grp):
                        sl = j % NSLOT
                        pb = 2 * (di % 2)
                        sgi = sg[di % 2]
                        sgk = ("sg", di % 2)
                        di += 1
                        pG = bank(pb)
                        pU = bank(pb + 1)
                        genN = None
                        if g == 0 and chi + 1 < len(chunks) and jl < len(chunks[chi + 1]):
                            t_ = chunks[chi + 1][jl]
                            n_ = ntok(t_)
                            c0_ = tok0(t_)
                            genN = norm_T_ops(t_, hn, hfT[:, :, c0_:c0_ + n_], ("hfT", t_), 4 + (t_ % 2),
                                              gain=GFb, gain_key="GFb")
                            for _ in range(3):
                                next(genN)

                        def mmgu(e, sl=sl, pG=pG, pU=pU):
                            ins = None
                            for kc in range(8):
                                ins = e.matmul(pG[:, 0:ncol], lhsT=WGU[sl][:, kc, 0:128],
                                               rhs=hfT[:, kc, cs:cs + ncol], start=(kc == 0), stop=(kc == 7))
                            for kc in range(8):
                                ins = e.matmul(pU[:, 0:ncol], lhsT=WGU[sl][:, kc, 128:256],
                                               rhs=hfT[:, kc, cs:cs + ncol], start=(kc == 0), stop=(kc == 7))
                            return ins
                        S.op("pe", mmgu, reads=[("WGU", sl)] + [("hfT", t) for t in ch],
                             writes=[pk(pb), pk(pb + 1)])
                        S.op("act", lambda e, pG=pG, sgi=sgi: e.activation(out=sgi[:, 0:ncol], in_=pG[:, 0:ncol],
                                                                          func=AF.Silu),
                             reads=[pk(pb)], writes=[sgk])
                        S.op("dve", lambda e, pU=pU, sgi=sgi, ab=ab, jl=jl: e.tensor_tensor(
                            out=ab[:, jl, 0:ncol], in0=sgi[:, 0:ncol], in1=pU[:, 0:ncol], op=ALU.mult),
                            reads=[sgk, pk(pb + 1)], writes=[abk])
                        if genN is not None:
                            for _ in genN:
                                pass
                    for t in ch:
                        n = ntok(t)
                        o0 = tok0(t) - cs
                        for hf in range(2):
                            pb = 4 + (hf + 2 * (t % 2))
                            pD = bank(pb)

                            def mmd(e, pD=pD, o0=o0, n=n, hf=hf, ab=ab):
                                ins = None
                                for jl, j in enumerate(grp):
                                    ins = e.matmul(pD[:n, :], lhsT=ab[:, jl, o0:o0 + n],
                                                   rhs=WD[j % NSLOT][:, hf * 512:(hf + 1) * 512],
                                                   start=(jl == 0), stop=(jl == len(grp) - 1))
                                return ins
                            S.op("pe", mmd, reads=[abk] + [("WD", j % NSLOT) for j in grp], writes=[pk(pb)])
                            S.op("dve", lambda e, pD=pD, n=n, t=t, hf=hf: e.tensor_tensor(
                                out=X[:n, t, hf * 512:(hf + 1) * 512], in0=X[:n, t, hf * 512:(hf + 1) * 512],
                                in1=pD[:n, :], op=ALU.add),
                                reads=[("X", t), pk(pb)], writes=[("X", t)])
                            if l == depth - 1 and g == len(groups) - 1 and hf == 1 and t >= 1:
                                S.dma("sp", "out", [(out_d[128 * (t - 1):128 * t, :], X[:, t, :])],
                                      reads=[("X", t)])
                                stored.add(t)
            S.barrier()

        if debug:
            S.final_wait("sp", "dbg")
        for t in range(1, NT):
            if t not in stored:
                S.dma("sp", "out", [(out_d[128 * (t - 1):128 * t, :], X[:, t, :])], reads=[("X", t)])
        S.final_wait("sp", "out")
        S.emit()
    return nc


def host_consts():
    pos = np.zeros((128, NT), dtype=np.float32)
    local = np.zeros((128, 2), dtype=np.float64)
    for t in range(NT):
        for p in range(128):
            pos[p, t] = tok0(t) + p if p < ntok(t) else 0
    p_idx = np.arange(128)
    local[:, 0] = p_idx
    local[:, 1] = np.minimum(112 + p_idx, 127)

    def tables(dim):
        inv = (10000.0 ** (-(np.arange(0, dim, 2, dtype=np.float32)) / dim)).astype(np.float32)
        ang = (pos[:, :, None] * inv[None, None, :]).astype(np.float32)
        return (np.cos(ang.astype(np.float64)).astype(np.float32).reshape(128, -1),
                np.sin(ang.astype(np.float64)).astype(np.float32).reshape(128, -1))
    cosR, sinR = tables(128)
    cosM, sinM = tables(64)
    lg = np.array([np.log(np.float32(g)) for g in GAMMA], dtype=np.float64)
    xi = np.zeros((128, 2, 4)); zeta = np.zeros((128, 2, 4)); dm = np.zeros((128, 2, 4, 128))
    nn = np.arange(128)
    for s in range(2):
        for h in range(4):
            xi[:, s, h] = np.exp((local[:, s] + 1.0) * lg[h])
            zeta[:, s, h] = np.exp((127.0 - local[:, s]) * lg[h]) * (128.0 ** -0.5)
            dm[:, s, h, :] = np.exp(-(local[:, s] + 1.0) * lg[h])[:, None] * (nn[None, :] >= nn[:, None])
    cm = (nn[None, :] >= nn[:, None]).astype(np.float32)
    return {
        "c_ident": np.eye(128, dtype=np.float32).astype(ml_dtypes.bfloat16),
        "c_cmask": cm.astype(ml_dtypes.bfloat16),
        "c_cosR": cosR, "c_sinR": sinR, "c_cosM": cosM, "c_sinM": sinM,
        "c_xi": xi.reshape(128, 8).astype(np.float32),
        "c_zeta": zeta.reshape(128, 8).astype(np.float32),
        "c_dmask": dm.reshape(128, -1).astype(np.float32),
    }


_NC_CACHE = {}


def kernel(**inputs):
    ins = {k: np.ascontiguousarray(np.asarray(v)) for k, v in inputs.items()}
    if "nc" not in _NC_CACHE:
        _NC_CACHE["nc"] = build()
    nc = _NC_CACHE["nc"]
    consts = host_consts()
    in_maps = []
    for b in range(8):
        m = {k: v for k, v in ins.items() if k != "x"}
        m["x"] = np.ascontiguousarray(ins["x"][b])
        m.update(consts)
        in_maps.append(m)
    res = run_bass_kernel_spmd(nc, in_maps, core_ids=list(range(8)))
    out = np.stack([np.asarray(r["out"]) for r in res.results], axis=0)
    return out.astype(np.float32)
```
